# Optimizing a Trainium2 kernel written in Bass

```python
import math
import jax, jax.numpy as jnp
from jax import lax
import numpy as np

D_MODEL = 1024
BATCH = 8
SEQ = 4096
DEPTH = 2

CHUNK = 64
Q_BLOCK = 128
CONV_DIM = D_MODEL // 2
CONV_WIDTH = 3
HG_HEADS = 4
HG_DK = 128
HG_DV = (D_MODEL // 2) // HG_HEADS
HG_FDIM = HG_HEADS * HG_DK
HG_WIDTH = HG_HEADS * HG_DV
SB_HEADS = 16
SB_HEAD_DIM = D_MODEL // SB_HEADS
SB_WIDTH = SB_HEADS * SB_HEAD_DIM
D_FF = 4 * D_MODEL
N_EVEN = (DEPTH + 1) // 2
N_ODD = DEPTH // 2
AB_IN = 3 * CONV_DIM + 2 * HG_FDIM + 2 * HG_WIDTH
AB_MIX = CONV_DIM + HG_WIDTH
AB_SPLITS = [CONV_DIM, 2 * CONV_DIM, 3 * CONV_DIM,
             3 * CONV_DIM + HG_FDIM, 3 * CONV_DIM + 2 * HG_FDIM,
             3 * CONV_DIM + 2 * HG_FDIM + HG_WIDTH]
EPS = 1e-6

kernel_name = "hybrid_chunk_causal_conv_hgrn2_stickbreak"


def rms_norm(x, g):
    xf = x.astype(jnp.float32)
    y = xf * lax.rsqrt(jnp.mean(xf * xf, axis=-1, keepdims=True) + EPS)
    return (y * g.astype(jnp.float32)).astype(x.dtype)


def causal_depthwise_conv(u, w):
    width, ch = w.shape
    return lax.conv_general_dilated(
        u, w[:, None, :], window_strides=(1,), padding=[(width - 1, 0)],
        dimension_numbers=('NWC', 'WIO', 'NWC'), feature_group_count=ch)


def hgrn2_chunkwise(q, k, v, log_f):
    B, S, H, Dk = q.shape
    Dv = v.shape[-1]
    n = S // CHUNK

    def to_chunks(t):
        return t.reshape(B, n, CHUNK, H, t.shape[-1]).transpose(1, 0, 3, 2, 4)

    qc, kc, vc, gc = to_chunks(q), to_chunks(k), to_chunks(v), to_chunks(log_f)
    bc = jnp.cumsum(gc, axis=3)
    causal = jnp.tril(jnp.ones((CHUNK, CHUNK), dtype=bool))

    def step(state, inp):
        q_, k_, v_, b_ = inp
        inter = jnp.einsum('bhtk,bhkv->bhtv', q_ * jnp.exp(b_), state)
        diff = jnp.where(causal[None, None, :, :, None],
                         b_[:, :, :, None, :] - b_[:, :, None, :, :], -jnp.inf)
        decay = jnp.exp(diff)
        scores = jnp.einsum('bhtk,bhsk,bhtsk->bhts', q_, k_, decay)
        intra = jnp.einsum('bhts,bhsv->bhtv', scores, v_)
        b_last = b_[:, :, -1:, :]
        new_state = (jnp.exp(b_last[:, :, 0, :])[..., None] * state
                     + jnp.einsum('bhsk,bhsv->bhkv', k_ * jnp.exp(b_last - b_), v_))
        return new_state, inter + intra

    init = jnp.zeros((B, H, Dk, Dv), jnp.float32)
    _, out = lax.scan(step, init, (qc, kc, vc, bc))
    return out.transpose(1, 0, 3, 2, 4).reshape(B, S, H, Dv)


def stick_breaking_attention(q, k, v):
    S = q.shape[2]
    scale = SB_HEAD_DIM ** -0.5
    outs = []
    for blk in range(S // Q_BLOCK):
        q0 = blk * Q_BLOCK
        end = q0 + Q_BLOCK
        qb = q[:, :, q0:end]
        kb = k[:, :, :end]
        vb = v[:, :, :end]
        z = jnp.einsum('bhqd,bhkd->bhqk', qb, kb) * scale
        qpos = q0 + jnp.arange(Q_BLOCK)
        kpos = jnp.arange(end)
        mask = (kpos[None, :] < qpos[:, None])[None, None]
        log_beta = jax.nn.log_sigmoid(z)
        log_one_minus = jnp.where(mask, log_beta - z, 0.0)
        later = lax.cumsum(log_one_minus, axis=3, reverse=True) - log_one_minus
        w = jnp.where(mask, jnp.exp(log_beta + later), 0.0)
        outs.append(jnp.einsum('bhqk,bhkd->bhqd', w, vb))
    return jnp.concatenate(outs, axis=2)


def mixer_conv_hgrn(h, w_in, conv_w, hg_norm, lower_bound, w_out):
    B, S, _ = h.shape
    u = h @ w_in
    a_b, a_c, a_h, hq, hf, hi, hg = jnp.split(u, AB_SPLITS, axis=-1)
    y_a = a_b * causal_depthwise_conv(a_c * a_h, conv_w)
    f = lower_bound + (1.0 - lower_bound) * jax.nn.sigmoid(hf.astype(jnp.float32))
    log_f = jnp.log(f)
    k_in = 1.0 - f
    heads_k = lambda t: t.reshape(B, S, HG_HEADS, HG_DK)
    o = hgrn2_chunkwise(heads_k(hq.astype(jnp.float32)), heads_k(k_in),
                        hi.astype(jnp.float32).reshape(B, S, HG_HEADS, HG_DV), heads_k(log_f))
    o = rms_norm(o, hg_norm.reshape(HG_HEADS, HG_DV))
    y_b = (o.reshape(B, S, HG_WIDTH) * jax.nn.silu(hg.astype(jnp.float32))).astype(h.dtype)
    return jnp.concatenate([y_a, y_b], axis=-1) @ w_out


def mixer_stick_breaking(h, w_qkv, q_norm, k_norm, w_out):
    B, S, _ = h.shape
    qkv = (h @ w_qkv).reshape(B, S, 3, SB_HEADS, SB_HEAD_DIM)
    q = rms_norm(qkv[:, :, 0], q_norm)
    k = rms_norm(qkv[:, :, 1], k_norm)
    v = qkv[:, :, 2]
    to_bhsd = lambda t: t.astype(jnp.float32).transpose(0, 2, 1, 3)
    o = stick_breaking_attention(to_bhsd(q), to_bhsd(k), to_bhsd(v))
    o = o.transpose(0, 2, 1, 3).reshape(B, S, SB_WIDTH).astype(h.dtype)
    return o @ w_out


def setup_inputs(seed: int = 0) -> dict:
    key = jax.random.key(seed)
    ks = jax.random.split(key, 20)
    nrm = lambda k, shape, s: jax.random.normal(k, shape, jnp.float32) * s
    gain = lambda k, shape: 1.0 + 0.02 * jax.random.normal(k, shape, jnp.float32)
    return {
        "x": nrm(ks[0], (BATCH, SEQ, D_MODEL), 1.0),
        "c": nrm(ks[1], (BATCH, D_MODEL), 1.0),
        "ada_w": nrm(ks[2], (DEPTH, D_MODEL, 6 * D_MODEL), 0.5 * D_MODEL ** -0.5),
        "ada_b": nrm(ks[3], (DEPTH, 6 * D_MODEL), 0.02),
        "norm_mix": gain(ks[4], (DEPTH, D_MODEL)),
        "norm_mlp": gain(ks[5], (DEPTH, D_MODEL)),
        "w_in_ab": nrm(ks[6], (N_EVEN, D_MODEL, AB_IN), D_MODEL ** -0.5),
        "conv_w": nrm(ks[7], (N_EVEN, CONV_WIDTH, CONV_DIM), CONV_WIDTH ** -0.5),
        "hg_norm": gain(ks[8], (N_EVEN, HG_WIDTH)),
        "lb_logits": nrm(ks[9], (DEPTH + 1, HG_FDIM), 0.1),
        "w_out_ab": nrm(ks[10], (N_EVEN, AB_MIX, D_MODEL), AB_MIX ** -0.5),
        "w_qkv": nrm(ks[11], (N_ODD, D_MODEL, 3 * SB_WIDTH), D_MODEL ** -0.5),
        "q_norm": gain(ks[12], (N_ODD, SB_HEAD_DIM)),
        "k_norm": gain(ks[13], (N_ODD, SB_HEAD_DIM)),
        "w_out_c": nrm(ks[14], (N_ODD, SB_WIDTH, D_MODEL), SB_WIDTH ** -0.5),
        "mlp_w1": nrm(ks[15], (DEPTH, D_MODEL, D_FF), D_MODEL ** -0.5),
        "mlp_w2": nrm(ks[16], (DEPTH, D_FF, D_MODEL), D_FF ** -0.5),
    }


def reference(x, c, ada_w, ada_b, norm_mix, norm_mlp, w_in_ab, conv_w, hg_norm,
              lb_logits, w_out_ab, w_qkv, q_norm, k_norm, w_out_c, mlp_w1, mlp_w2):
    c_act = jax.nn.silu(c)
    lower_bounds = jnp.cumsum(jax.nn.softmax(lb_logits.astype(jnp.float32), axis=0), axis=0)
    for layer in range(DEPTH):
        mod = c_act @ ada_w[layer] + ada_b[layer]
        shift1, scale1, gate1, shift2, scale2, gate2 = jnp.split(mod[:, None, :], 6, axis=-1)
        h = rms_norm(x, norm_mix[layer]) * (1.0 + scale1) + shift1
        j = layer // 2
        if layer % 2 == 0:
            y = mixer_conv_hgrn(h, w_in_ab[j], conv_w[j], hg_norm[j],
                                lower_bounds[layer], w_out_ab[j])
        else:
            y = mixer_stick_breaking(h, w_qkv[j], q_norm[j], k_norm[j], w_out_c[j])
        x = x + gate1 * y
        h = rms_norm(x, norm_mlp[layer]) * (1.0 + scale2) + shift2
        x = x + gate2 * (jnp.square(jax.nn.relu(h @ mlp_w1[layer])) @ mlp_w2[layer])
    return x
```

```python
import contextlib
import numpy as np
import ml_dtypes
import concourse.bass as bass
import concourse.mybir as mybir
from concourse.bass_utils import run_bass_kernel_spmd

F32 = mybir.dt.float32
BF16 = mybir.dt.bfloat16
AF = mybir.ActivationFunctionType
ALU = mybir.AluOpType

COMPUTE = ("pe", "act", "dve", "pool")


class Op:
    __slots__ = ("eng", "fn", "deps", "signal", "sigval", "is_dma", "semkey", "dmaval")

    def __init__(self, eng, fn, is_dma=False, semkey=None):
        self.eng = eng
        self.fn = fn
        self.deps = []
        self.signal = False
        self.sigval = 0
        self.is_dma = is_dma
        self.semkey = semkey
        self.dmaval = 0


class Prog:
    def __init__(self, nc):
        self.nc = nc
        self.ops = {e: [] for e in ("pe", "act", "dve", "pool", "sp")}
        self.last_w = {}
        self.readers = {}
        self.dma_counts = {}
        self.final_dma = []
        self.bar = None
        self.bar_done = set()
        self.phase_dmas = []

    def barrier(self):
        deps = []
        for e in COMPUTE:
            if self.ops[e]:
                for o in reversed(self.ops[e]):
                    if not o.is_dma:
                        deps.append(o)
                        break
        deps.extend(self.phase_dmas)
        self.phase_dmas = []
        self.bar = deps
        self.bar_done = set()

    def _hazards(self, op, reads, writes):
        deps = []
        if self.bar is not None and op.eng not in self.bar_done:
            self.bar_done.add(op.eng)
            deps.extend(self.bar)
        for b in reads:
            w = self.last_w.get(b)
            if w is not None:
                deps.append(w)
        for b in writes:
            w = self.last_w.get(b)
            if w is not None:
                deps.append(w)
            deps.extend(self.readers.get(b, ()))
        seen = set()
        out = []
        dkeys = {}
        for d in deps:
            if id(d) in seen or d is op:
                continue
            seen.add(id(d))
            if d.is_dma:
                dkeys[d.semkey] = 16 * self.dma_counts[d.semkey]
                continue
            if d.eng == "pe" and op.eng == "pe" and not op.is_dma:
                continue
            out.append(d)
        for k, v in dkeys.items():
            out.append(("dma", k, v))
        op.deps = out
        for d in out:
            if not isinstance(d, tuple):
                d.signal = True
        for b in reads:
            lst = self.readers.setdefault(b, [])
            if not op.is_dma:
                for i, r in enumerate(lst):
                    if (not r.is_dma) and r.eng == op.eng:
                        lst[i] = op
                        break
                else:
                    lst.append(op)
            else:
                lst.append(op)
        for b in writes:
            self.last_w[b] = op
            self.readers[b] = []

    def op(self, eng, fn, reads=(), writes=()):
        o = Op(eng, fn)
        self._hazards(o, reads, writes)
        self.ops[eng].append(o)
        return o

    def dma(self, q, out, in_, reads=(), writes=(), key=None, final=False):
        o = Op(q, (out, in_), is_dma=True, semkey=key)
        self._hazards(o, reads, writes)
        self.dma_counts[key] = self.dma_counts.get(key, 0) + 1
        o.dmaval = 16 * self.dma_counts[key]
        self.ops[q].append(o)
        self.phase_dmas.append(o)
        if final:
            self.final_dma.append(o)
        return o

    def emit(self):
        nc = self.nc
        with contextlib.ExitStack() as es:
            sems = {}
            for e in COMPUTE:
                sems[e] = es.enter_context(nc.semaphore("s_" + e))
            dsem = {}
            for k in self.dma_counts:
                dsem[k] = es.enter_context(nc.semaphore("d_%s" % (k,)))
            for e in COMPUTE:
                c = 0
                for o in self.ops[e]:
                    if o.signal:
                        c += 1
                        o.sigval = c
            block = es.enter_context(nc.Block())
            engmap = {"pe": "tensor", "act": "scalar", "dve": "vector", "pool": "gpsimd", "sp": "sync"}

            def run(ename, eobj):
                waited = {}
                for o in self.ops[ename]:
                    for d in o.deps:
                        if isinstance(d, tuple):
                            k = ("dma", d[1])
                            v = d[2]
                            s = dsem[d[1]]
                        else:
                            k = d.eng
                            v = d.sigval
                            s = sems[d.eng]
                        if waited.get(k, 0) >= v:
                            continue
                        waited[k] = v
                        eobj.wait_ge(s, v)
                    if o.is_dma:
                        out, in_ = o.fn
                        eobj.dma_start(out=out, in_=in_).then_inc(dsem[o.semkey], 16)
                    else:
                        ins = o.fn(eobj)
                        if o.signal:
                            ins.then_inc(sems[ename], 1)
                if ename == "sp":
                    for kk in sorted({o.semkey for o in self.final_dma}):
                        v = 16 * self.dma_counts[kk]
                        if waited.get(("dma", kk), 0) < v:
                            waited[("dma", kk)] = v
                            eobj.wait_ge(dsem[kk], v)

            for ename in ("pe", "act", "dve", "pool", "sp"):
                deco = getattr(block, engmap[ename])

                def body(eobj, ename=ename):
                    run(ename, eobj)
                deco(body)


D = 1024
S = 4096
T = 256
NT = S // T
EPS = 1e-6

C_ID = 0
C_J = 128
C_TRIL = 256
C_M64 = 384
C_SCAN = 448
C_ONES = C_SCAN + 1024
C_BD = C_ONES + 512
C_EPS = C_BD + 128
NCF = C_EPS + 1
V_B = 0
V_NM = 96
V_NL = 112
V_CW = 128
V_HG = 140
V_LB = 144
V_QN = 156
V_KN = 157
NV = 158


def make_consts():
    c = np.zeros((128, NCF), np.float32)
    c[:, C_ID:C_ID + 128] = np.eye(128, dtype=np.float32)
    c[:, C_J:C_J + 128] = np.eye(128, dtype=np.float32)[::-1]
    p = np.arange(128)
    c[:, C_TRIL:C_TRIL + 128] = (p[None, :] <= p[:, None]).astype(np.float32)
    s64 = np.arange(64)
    c[:64, C_M64:C_M64 + 64] = (s64[:, None] <= s64[None, :]).astype(np.float32)
    m = np.ones(1024, np.float32)
    m[::64] = 0.0
    c[:, C_SCAN:C_SCAN + 1024] = m[None, :]
    c[:, C_ONES:C_ONES + 512] = 1.0
    bd = np.zeros((128, 128), np.float32)
    bd[:64, :64] = 1.0
    bd[64:, 64:] = 1.0
    c[:, C_BD:C_BD + 128] = bd
    c[:, C_EPS] = EPS
    return c


def make_vecs(ada_b, norm_mix, norm_mlp, conv_w, hg_norm, lb_logits, q_norm, k_norm):
    v = np.zeros((128, NV), np.float32)
    v[:, V_B:V_B + 96] = ada_b.reshape(2, 48, 128).transpose(2, 0, 1).reshape(128, 96)
    v[:, V_NM:V_NM + 16] = norm_mix.reshape(2, 8, 128).transpose(2, 0, 1).reshape(128, 16)
    v[:, V_NL:V_NL + 16] = norm_mlp.reshape(2, 8, 128).transpose(2, 0, 1).reshape(128, 16)
    v[:, V_CW:V_CW + 12] = conv_w.reshape(3, 4, 128).transpose(2, 0, 1).reshape(128, 12)
    v[:, V_HG:V_HG + 4] = hg_norm.reshape(4, 128).T
    v[:, V_LB:V_LB + 12] = lb_logits.reshape(3, 4, 128).transpose(2, 0, 1).reshape(128, 12)
    v[:, V_QN] = np.tile(q_norm.reshape(64), 2)
    v[:, V_KN] = np.tile(k_norm.reshape(64), 2)
    return v


def build(phases="MABCDE", dbg=False, first_only=False):
    nc = bass.Bass("TRN2", target_bir_lowering=False)

    def dram(name, shape, dt, kind):
        return nc.dram_tensor(name, shape, dt, kind=kind).ap()

    x_in = dram("x", [S, D], F32, "ExternalInput")
    c_in = dram("cfm", [128, 8], F32, "ExternalInput")
    constf = dram("constf", [128, NCF], F32, "ExternalInput")
    vecs_in = dram("vecs", [128, NV], F32, "ExternalInput")
    ada_w = dram("ada_w", [2, 1024, 6144], F32, "ExternalInput")
    w_in = dram("w_in_ab", [1024, 3584], F32, "ExternalInput")
    w_oab = dram("w_out_ab", [1024, 1024], F32, "ExternalInput")
    w_qkv = dram("w_qkv", [1024, 3072], F32, "ExternalInput")
    w_oc = dram("w_out_c", [1024, 1024], F32, "ExternalInput")
    w1_in = dram("mlp_w1", [2, 1024, 4096], F32, "ExternalInput")
    w2_in = dram("mlp_w2", [2, 4096, 1024], F32, "ExternalInput")
    out = dram("out", [S, D], F32, "ExternalOutput")
    sk = "ExternalOutput" if dbg else "Internal"
    x1T = dram("x1T", [8, 128, S], F32, sk)
    x2T = dram("x2T", [8, 128, S], F32, sk)
    x3T = dram("x3T", [8, 128, S], F32, sk)
    qTs = dram("qTs", [8, 128, S], BF16, sk)
    kTs = dram("kTs", [8, 128, S], BF16, sk)
    vS = dram("vS", [S, 1024], BF16, sk)
    oTs = dram("oTs", [8, 128, S], BF16, sk)
    mod_dbg = dram("mod_dbg", [128, 96], F32, "ExternalOutput") if dbg else None

    nfo = int(first_only)
    tilesAB = list(range(nfo)) if first_only else list(range(NT))
    tilesCDE = list(range(NT - nfo, NT)) if first_only else list(range(NT))
    qbs = list(range(32 - 2 * nfo, 32)) if first_only else list(range(32))

    P = Prog(nc)
    SB_BASE = 16512
    SB_LIMIT = 229300
    sb_off = [SB_BASE]
    uid = [0]

    def SB(name, shape, dt=F32):
        nbytes = int(np.prod(shape[1:])) * (4 if dt == F32 else 2)
        off = (sb_off[0] + 31) // 32 * 32
        sb_off[0] = off + nbytes
        assert sb_off[0] <= SB_LIMIT, (name, sb_off[0])
        uid[0] += 1
        return nc.alloc_sbuf_tensor_at("%s_%d" % (name, uid[0]), shape, dt, offset=off)

    banks = [nc.alloc_psum_tensor("pb%d" % i, [128, 512], F32) for i in range(8)]
    banks_bf = [b.bitcast(BF16) for b in banks]
    rot = {"lst": list(range(8)), "i": 0}

    def set_rot(lst):
        rot["lst"] = list(lst)
        rot["i"] = 0

    def bank():
        b = rot["lst"][rot["i"] % len(rot["lst"])]
        rot["i"] += 1
        return b

    def BN(b):
        return "pb%d" % b

    cst = SB("cst", [128, NCF])
    vec = SB("vec", [128, NV])
    csb = SB("csb", [128, 8])
    cact = SB("cact", [128, 8])
    mod = SB("mod", [128, 96])
    A1 = SB("A1", [128, 16])
    A2 = SB("A2", [128, 16])
    lbv = SB("lbv", [128, 4])
    omlb = SB("omlb", [128, 4])
    lbt = SB("lbt", [128, 12])
    lbs = SB("lbs", [128, 4])
    id_bf = SB("id_bf", [128, 128], BF16)
    ones_bf = SB("ones_bf", [128, 128], BF16)
    bd_bf = SB("bd_bf", [128, 128], BF16)
    zero_bf = SB("zero_bf", [128, 128], BF16)
    persist_end = sb_off[0]

    ident = cst[:, C_ID:C_ID + 128]
    jmat = cst[:, C_J:C_J + 128]
    tril = cst[:, C_TRIL:C_TRIL + 128]
    onesf = cst[:, C_ONES:C_ONES + 512]
    eps_ap = cst[:, C_EPS:C_EPS + 1]

    P.dma("sp", cst[:], constf, writes=["cst"], key="c0")
    P.dma("sp", vec[:], vecs_in, writes=["vec"], key="c0")
    P.dma("sp", csb[:], c_in, writes=["csb"], key="c0")
    P.op("dve", lambda e: e.tensor_copy(id_bf[:], ident), reads=["cst"], writes=["id_bf"])
    P.op("dve", lambda e: e.tensor_copy(ones_bf[:], cst[:, C_ONES:C_ONES + 128]), reads=["cst"], writes=["ones_bf"])
    P.op("dve", lambda e: e.tensor_copy(bd_bf[:], cst[:, C_BD:C_BD + 128]), reads=["cst"], writes=["bd_bf"])
    P.op("pool", lambda e: e.memset(zero_bf[:], 0.0), writes=["zero_bf"])

    def shift1(l, c):
        return mod[:, l * 48 + 0 + c: l * 48 + 0 + c + 1]

    def gate1(l, c):
        return mod[:, l * 48 + 16 + c: l * 48 + 16 + c + 1]

    def shift2(l, c):
        return mod[:, l * 48 + 24 + c: l * 48 + 24 + c + 1]

    def gate2(l, c):
        return mod[:, l * 48 + 40 + c: l * 48 + 40 + c + 1]

    def _phM():
        sb_off[0] = persist_end
        awb = [SB("awb0", [128, 6144]), SB("awb1", [128, 6144])]
        P.op("act", lambda e: e.activation(cact[:], csb[:], AF.Silu), reads=["csb"], writes=["cact"])
        pm = banks[0]
        P.op("pe", lambda e: e.matmul(pm[:, 0:96], zero_bf[:], zero_bf[:, 0:96], start=True, stop=True),
             reads=["zero_bf"], writes=["pb0"])
        for l in range(2):
            for kc in range(8):
                bi = kc % 2
                P.dma("sp", awb[bi][:], ada_w[l, kc * 128:(kc + 1) * 128, :], writes=["awb%d" % bi], key="aw%d" % bi)
                for j in range(48):
                    P.op("pe", lambda e, l=l, kc=kc, j=j, bi=bi: e.matmul(
                        pm[:, l * 48 + j: l * 48 + j + 1], awb[bi][:, j * 128:(j + 1) * 128], cact[:, kc:kc + 1],
                        start=False, stop=(l == 1 and kc == 7 and j == 47), skip_group_check=True),
                        reads=["awb%d" % bi, "cact"], writes=["pb0"])
        P.op("dve", lambda e: e.tensor_tensor(mod[:], pm[:, 0:96], vec[:, V_B:V_B + 96], ALU.add),
             reads=["pb0", "vec"], writes=["mod"])
        for l in range(2):
            P.op("dve", lambda e, l=l: e.scalar_tensor_tensor(
                A1[:, l * 8:(l + 1) * 8], mod[:, l * 48 + 8: l * 48 + 16], 1.0, vec[:, V_NM + l * 8: V_NM + (l + 1) * 8],
                ALU.add, ALU.mult), reads=["mod", "vec"], writes=["A1"])
            P.op("dve", lambda e, l=l: e.scalar_tensor_tensor(
                A2[:, l * 8:(l + 1) * 8], mod[:, l * 48 + 32: l * 48 + 40], 1.0, vec[:, V_NL + l * 8: V_NL + (l + 1) * 8],
                ALU.add, ALU.mult), reads=["mod", "vec"], writes=["A2"])
        P.op("act", lambda e: e.activation(lbt[:], vec[:, V_LB:V_LB + 12], AF.Exp), reads=["vec"], writes=["lbt"])
        P.op("dve", lambda e: e.tensor_tensor(lbs[:], lbt[:, 0:4], lbt[:, 4:8], ALU.add), reads=["lbt"], writes=["lbs"])
        P.op("dve", lambda e: e.tensor_tensor(lbs[:], lbs[:], lbt[:, 8:12], ALU.add), reads=["lbt", "lbs"], writes=["lbs"])
        P.op("dve", lambda e: e.reciprocal(lbs[:], lbs[:]), reads=["lbs"], writes=["lbs"])
        P.op("dve", lambda e: e.tensor_tensor(lbv[:], lbt[:, 0:4], lbs[:], ALU.mult), reads=["lbt", "lbs"], writes=["lbv"])
        P.op("dve", lambda e: e.tensor_scalar(omlb[:], lbv[:], -1.0, 1.0, ALU.mult, ALU.add), reads=["lbv"], writes=["omlb"])
        if dbg:
            P.dma("sp", mod_dbg, mod[:], reads=["mod"], key="dbg", final=True)
        P.barrier()

    if "M" in phases:
        _phM()

    def load_w(dst, dst_name, src, kcn, key):
        for kc in range(kcn):
            P.dma("pool", dst[:, kc, :], src[kc * 128:(kc + 1) * 128, :], writes=[dst_name], key=key)

    def rms_mod(xT, xT_name, hT, hT_name, sq, xn, rs, Avec, l, shiftf):
        P.op("act", lambda e: e.activation(sq[:].rearrange("p c t -> p (c t)"), xT[:].rearrange("p c t -> p (c t)"), AF.Square),
             reads=[xT_name], writes=["sq"])
        b = bank()
        for c in range(8):
            P.op("pe", lambda e, c=c, b=b: e.matmul(banks[b][:, 0:T], ones_bf[:], sq[:, c, :], start=(c == 0), stop=(c == 7)),
                 reads=["sq", "ones_bf"], writes=[BN(b)])
        P.op("act", lambda e, b=b: e.activation(rs[:], banks[b][:, 0:T], AF.Sqrt, bias=eps_ap, scale=1.0 / D),
             reads=[BN(b), "cst"], writes=["rs"])
        P.op("dve", lambda e: e.reciprocal(rs[:], rs[:]), reads=["rs"], writes=["rs"])
        rs_b = bass.AP(rs, 0, [[T, 128], [0, 8], [1, T]])
        P.op("dve", lambda e: e.tensor_tensor(xn[:], xT[:], rs_b, ALU.mult), reads=[xT_name, "rs"], writes=["xn"])
        for c in range(8):
            if c % 2 == 0:
                P.op("pool", lambda e, c=c: e.tensor_scalar(hT[:, c, :], xn[:, c, :], Avec[:, l * 8 + c: l * 8 + c + 1], shiftf(l, c),
                                                           ALU.mult, ALU.add), reads=["xn", "mod", "A1", "A2"], writes=[hT_name])
            else:
                P.op("act", lambda e, c=c: e.activation(hT[:, c, :], xn[:, c, :], AF.Identity, bias=shiftf(l, c),
                                                        scale=Avec[:, l * 8 + c: l * 8 + c + 1]),
                     reads=["xn", "mod", "A1", "A2"], writes=[hT_name])

    def mlp_tile(l, hT, hT_name, w1, w2, hid, rt, xT, xT_name):
        for jp in range(16):
            b = bank()
            for jj in range(2):
                j = jp * 2 + jj
                for kc in range(8):
                    P.op("pe", lambda e, j=j, jj=jj, kc=kc, b=b: e.matmul(
                        banks[b][:, jj * T:(jj + 1) * T], w1[:, kc, j * 128:(j + 1) * 128], hT[:, kc, :],
                        start=(kc == 0), stop=(kc == 7)), reads=[hT_name, "w1"], writes=[BN(b)])
            ri = jp % 2
            P.op("act", lambda e, b=b, ri=ri: e.activation(rt[ri][:], banks[b][:], AF.Relu), reads=[BN(b)], writes=["rt%d" % ri])
            eng = "pool" if jp % 2 == 0 else "dve"
            P.op(eng, lambda e, jp=jp, ri=ri: e.tensor_tensor(
                hid[:, 2 * jp:2 * jp + 2, :].rearrange("p c t -> p (c t)"), rt[ri][:], rt[ri][:], ALU.mult),
                reads=["rt%d" % ri], writes=["hid"])
        for ocp in range(4):
            b = bank()
            for oo in range(2):
                oc = ocp * 2 + oo
                for j in range(32):
                    P.op("pe", lambda e, oc=oc, oo=oo, j=j, b=b: e.matmul(
                        banks[b][:, oo * T:(oo + 1) * T], w2[:, j, oc * 128:(oc + 1) * 128], hid[:, j, :],
                        start=(j == 0), stop=(j == 31)), reads=["hid", "w2"], writes=[BN(b)])
            for oo in range(2):
                oc = ocp * 2 + oo
                P.op("dve", lambda e, oc=oc, oo=oo, b=b: e.scalar_tensor_tensor(
                    xT[:, oc, :], banks[b][:, oo * T:(oo + 1) * T], gate2(l, oc), xT[:, oc, :], ALU.mult, ALU.add),
                    reads=[BN(b), xT_name, "mod"], writes=[xT_name])

    def fm(dr, t0):
        return dr[:, :, t0:t0 + T].rearrange("c p t -> p c t")

    def _phA():
        sb_off[0] = persist_end
        set_rot([0, 1, 2, 3])
        B_O, B_SC, B_KT, B_DS = 7, 6, 5, 4
        w_in_sb = SB("w_in_sb", [128, 8, 3584], BF16)
        w_oab_sb = SB("w_oab_sb", [128, 8, 1024], BF16)
        xtm = [SB("xtm0", [128, 2, 1024]), SB("xtm1", [128, 2, 1024])]
        xT = SB("xT", [128, 8, T])
        sq = SB("sq", [128, 8, T], BF16)
        xn = SB("xn", [128, 8, T])
        rs = SB("rs", [128, T])
        hT = SB("hT", [128, 8, T], BF16)
        ah = [SB("ah0", [128, T]), SB("ah1", [128, T])]
        pbuf = SB("pbuf", [128, 4, T + 2])
        ct = [SB("ct0", [128, T]), SB("ct1", [128, T])]
        ya = SB("ya", [128, 4, T], BF16)
        fb = SB("fb", [128, 4, T])
        lf = SB("lf", [128, 4, T])
        bb = SB("bb", [128, 4, T])
        eb = SB("eb", [128, 4, T])
        kin = SB("kin", [128, 4, T])
        qe = SB("qe", [128, 4, T], BF16)
        ke = SB("ke", [128, 4, T], BF16)
        kd = SB("kd", [128, 4, T], BF16)
        vtm = SB("vtm", [64, 4, 512], BF16)
        Sst = SB("Sst", [128, 4, 128])
        Sbf = SB("Sbf", [128, 4, 128], BF16)
        scm = SB("scm", [64, 4, 64], BF16)
        kdT = SB("kdT", [64, 4, 128], BF16)
        osb = SB("osb", [128, 4, T])
        osq = SB("osq", [128, 4, T], BF16)
        sg = SB("sg", [128, 4, T])
        ors = SB("ors", [128, 4, T])
        ytmp = SB("ytmp", [128, 4, T])
        yb = SB("yb", [128, 4, T], BF16)

        load_w(w_in_sb, "w_in_sb", w_in, 8, "wA")
        load_w(w_oab_sb, "w_oab_sb", w_oab, 8, "wA")
        P.op("pool", lambda e: e.memset(pbuf[:], 0.0), writes=["pbuf"])
        P.op("pool", lambda e: e.memset(Sst[:], 0.0), writes=["Sst"])
        P.op("pool", lambda e: e.memset(Sbf[:], 0.0), writes=["Sbf"])

        def cw(w, j):
            return vec[:, V_CW + w * 4 + j: V_CW + w * 4 + j + 1]

        def load_x(i):
            s = i % 2
            P.dma("sp", xtm[s][:], x_in[i * T:(i + 1) * T, :].rearrange("(s p) f -> p s f", p=128),
                  writes=["xtm%d" % s], key="xtm%d" % s)

        def proj(col0, b, half):
            for kc in range(8):
                P.op("pe", lambda e, kc=kc: e.matmul(banks[b][:, half * T:(half + 1) * T], w_in_sb[:, kc, col0:col0 + 128], hT[:, kc, :],
                                                    start=(kc == 0), stop=(kc == 7)), reads=["hT", "w_in_sb"], writes=[BN(b)])

        load_x(tilesAB[0])
        for idx, i in enumerate(tilesAB):
            s = i % 2
            if idx + 1 < len(tilesAB):
                load_x(tilesAB[idx + 1])
            for cp in range(4):
                b = bank()
                for cc in range(2):
                    c = cp * 2 + cc
                    for su in range(2):
                        P.op("pe", lambda e, c=c, cc=cc, su=su, b=b, s=s: e.transpose(
                            banks[b][:, cc * T + su * 128: cc * T + (su + 1) * 128], xtm[s][:, su, c * 128:(c + 1) * 128], ident),
                            reads=["xtm%d" % s, "cst"], writes=[BN(b)])
                eng = "act" if cp % 2 == 0 else "dve"
                if eng == "act":
                    P.op("act", lambda e, cp=cp, b=b: e.copy(xT[:, 2 * cp:2 * cp + 2, :].rearrange("p c t -> p (c t)"), banks[b][:]),
                         reads=[BN(b)], writes=["xT"])
                else:
                    P.op("dve", lambda e, cp=cp, b=b: e.tensor_copy(xT[:, 2 * cp:2 * cp + 2, :].rearrange("p c t -> p (c t)"), banks[b][:]),
                         reads=[BN(b)], writes=["xT"])
            rms_mod(xT, "xT", hT, "hT", sq, xn, rs, A1, 0, shift1)
            P.op("pool", lambda e: e.tensor_copy(pbuf[:, :, 0:2], pbuf[:, :, T:T + 2]), reads=["pbuf"], writes=["pbuf"])
            for j in range(4):
                b = bank()
                proj(512 + j * 128, b, 0)
                proj(1024 + j * 128, b, 1)
                a = j % 2
                P.op("act", lambda e, b=b, a=a: e.copy(ah[a][:], banks[b][:, T:2 * T]), reads=[BN(b)], writes=["ah%d" % a])
                P.op("dve", lambda e, b=b, a=a, j=j: e.tensor_tensor(pbuf[:, j, 2:T + 2], banks[b][:, 0:T], ah[a][:], ALU.mult),
                     reads=[BN(b), "ah%d" % a], writes=["pbuf"])
                P.op("pool", lambda e, j=j, a=a: e.tensor_scalar(ct[a][:], pbuf[:, j, 0:T], cw(0, j), None, ALU.mult),
                     reads=["pbuf", "vec"], writes=["ct%d" % a])
                P.op("dve", lambda e, j=j, a=a: e.scalar_tensor_tensor(ct[a][:], pbuf[:, j, 1:T + 1], cw(1, j), ct[a][:], ALU.mult, ALU.add),
                     reads=["pbuf", "vec", "ct%d" % a], writes=["ct%d" % a])
                P.op("dve", lambda e, j=j, a=a: e.scalar_tensor_tensor(ct[a][:], pbuf[:, j, 2:T + 2], cw(2, j), ct[a][:], ALU.mult, ALU.add),
                     reads=["pbuf", "vec", "ct%d" % a], writes=["ct%d" % a])
                b2 = bank()
                proj(j * 128, b2, 0)
                P.op("dve", lambda e, j=j, a=a, b2=b2: e.tensor_tensor(ya[:, j, :], banks[b2][:, 0:T], ct[a][:], ALU.mult),
                     reads=[BN(b2), "ct%d" % a], writes=["ya"])
            for hp in range(2):
                b = bank()
                proj(2048 + (2 * hp) * 128, b, 0)
                proj(2048 + (2 * hp + 1) * 128, b, 1)
                P.op("act", lambda e, b=b, hp=hp: e.activation(fb[:, 2 * hp:2 * hp + 2, :].rearrange("p c t -> p (c t)"), banks[b][:], AF.Sigmoid),
                     reads=[BN(b)], writes=["fb"])
            for h in range(4):
                P.op("pool", lambda e, h=h: e.tensor_scalar(fb[:, h, :], fb[:, h, :], omlb[:, h:h + 1], lbv[:, h:h + 1], ALU.mult, ALU.add),
                     reads=["fb", "lbv", "omlb"], writes=["fb"])
            fl = lambda t: t[:].rearrange("p c t -> p (c t)")
            P.op("act", lambda e: e.activation(fl(lf), fl(fb), AF.Ln), reads=["fb"], writes=["lf"])
            P.op("dve", lambda e: e.tensor_tensor_scan(fl(bb), cst[:, C_SCAN:C_SCAN + 4 * T], fl(lf), 0.0, ALU.mult, ALU.add),
                 reads=["lf", "cst"], writes=["bb"])
            P.op("pool", lambda e: e.tensor_scalar(fl(kin), fl(fb), -1.0, 1.0, ALU.mult, ALU.add), reads=["fb"], writes=["kin"])
            P.op("act", lambda e: e.activation(fl(eb), fl(bb), AF.Exp), reads=["bb"], writes=["eb"])
            P.op("act", lambda e: e.activation(fl(lf), fl(bb), AF.Exp, scale=-1.0), reads=["bb"], writes=["lf"])
            P.op("pool", lambda e: e.tensor_tensor(fl(kin), fl(kin), fl(lf), ALU.mult), reads=["kin", "lf"], writes=["kin"])
            P.op("pool", lambda e: e.tensor_copy(fl(ke), fl(kin)), reads=["kin"], writes=["ke"])
            for h in range(4):
                ebl = bass.AP(eb, h * T + 63, [[4 * T, 128], [64, T // 64], [0, 64]])
                P.op("dve", lambda e, h=h, ebl=ebl: e.tensor_tensor(
                    kd[:, h, :].rearrange("p (c t) -> p c t", t=64), kin[:, h, :].rearrange("p (c t) -> p c t", t=64), ebl, ALU.mult),
                    reads=["kin", "eb"], writes=["kd"])
            for hp in range(2):
                b = bank()
                proj(1536 + (2 * hp) * 128, b, 0)
                proj(1536 + (2 * hp + 1) * 128, b, 1)
                P.op("dve", lambda e, b=b, hp=hp: e.tensor_tensor(
                    qe[:, 2 * hp:2 * hp + 2, :].rearrange("p c t -> p (c t)"), banks[b][:],
                    eb[:, 2 * hp:2 * hp + 2, :].rearrange("p c t -> p (c t)"), ALU.mult), reads=[BN(b), "eb"], writes=["qe"])
            for hp in range(2):
                b = bank()
                proj(3072 + (2 * hp) * 128, b, 0)
                proj(3072 + (2 * hp + 1) * 128, b, 1)
                P.op("act", lambda e, b=b, hp=hp: e.activation(sg[:, 2 * hp:2 * hp + 2, :].rearrange("p c t -> p (c t)"), banks[b][:], AF.Silu),
                     reads=[BN(b)], writes=["sg"])
            for cc in range(4):
                b = bank()
                for kc in range(8):
                    P.op("pe", lambda e, kc=kc, cc=cc, b=b: e.matmul(banks[b][0:64, :], hT[:, kc, cc * 64:(cc + 1) * 64], w_in_sb[:, kc, 2560:3072],
                                                                   start=(kc == 0), stop=(kc == 7)), reads=["hT", "w_in_sb"], writes=[BN(b)])
                P.op("act", lambda e, cc=cc, b=b: e.copy(vtm[:, cc, :], banks[b][0:64, :]), reads=[BN(b)], writes=["vtm"])
            pO, pSC, pKT, pDS = banks[B_O], banks[B_SC], banks_bf[B_KT], banks[B_DS]
            m64 = bass.AP(cst, C_M64, [[NCF, 64], [0, 4], [1, 64]])
            for cc in range(4):
                tsl = slice(cc * 64, (cc + 1) * 64)
                for h in range(4):
                    P.op("pe", lambda e, h=h, tsl=tsl: e.matmul(pSC[0:64, h * 64:(h + 1) * 64], ke[:, h, tsl], qe[:, h, tsl], start=True, stop=True),
                         reads=["ke", "qe"], writes=[BN(B_SC)])
                P.op("dve", lambda e: e.tensor_tensor(scm[:], pSC[0:64, 0:256].rearrange("p (h t) -> p h t", h=4), m64, ALU.mult),
                     reads=[BN(B_SC), "cst"], writes=["scm"])
                for h in range(4):
                    P.op("pe", lambda e, h=h, tsl=tsl: e.matmul(pO[:, h * 64:(h + 1) * 64], Sbf[:, h, :], qe[:, h, tsl], start=True, stop=False),
                         reads=["Sbf", "qe"], writes=[BN(B_O)])
                    P.op("pe", lambda e, h=h, cc=cc: e.matmul(pO[:, h * 64:(h + 1) * 64], vtm[:, cc, h * 128:(h + 1) * 128], scm[:, h, :],
                                                             start=False, stop=True),
                         reads=["vtm", "scm"], writes=[BN(B_O)])
                P.op("act", lambda e, tsl=tsl: e.copy(osb[:, :, tsl], pO[:, 0:256].rearrange("p (h t) -> p h t", h=4)),
                     reads=[BN(B_O)], writes=["osb"])
                for h in range(4):
                    P.op("pe", lambda e, h=h, tsl=tsl: e.transpose(pKT[0:64, h * 128:(h + 1) * 128], kd[:, h, tsl], id_bf[:]),
                         reads=["kd", "id_bf"], writes=[BN(B_KT)])
                P.op("act", lambda e: e.copy(kdT[:].rearrange("p h k -> p (h k)"), pKT[0:64, 0:512]), reads=[BN(B_KT)], writes=["kdT"])
                for h in range(4):
                    P.op("pe", lambda e, h=h, cc=cc: e.matmul(pDS[:, h * 128:(h + 1) * 128], kdT[:, h, :], vtm[:, cc, h * 128:(h + 1) * 128],
                                                             start=True, stop=True), reads=["kdT", "vtm"], writes=[BN(B_DS)])
                ebl2 = bass.AP(eb, cc * 64 + 63, [[4 * T, 128], [T, 4], [0, 128]])
                P.op("dve", lambda e, ebl2=ebl2: e.tensor_tensor(Sst[:], Sst[:], ebl2, ALU.mult), reads=["Sst", "eb"], writes=["Sst"])
                P.op("dve", lambda e: e.tensor_tensor(Sst[:].rearrange("p h v -> p (h v)"), Sst[:].rearrange("p h v -> p (h v)"), pDS[:], ALU.add),
                     reads=["Sst", BN(B_DS)], writes=["Sst"])
                P.op("act", lambda e: e.copy(Sbf[:].rearrange("p h v -> p (h v)"), Sst[:].rearrange("p h v -> p (h v)")),
                     reads=["Sst"], writes=["Sbf"])
            P.op("act", lambda e: e.activation(fl(osq), fl(osb), AF.Square), reads=["osb"], writes=["osq"])
            for hp in range(2):
                b = bank()
                for hh in range(2):
                    h = 2 * hp + hh
                    P.op("pe", lambda e, h=h, hh=hh, b=b: e.matmul(banks[b][:, hh * T:(hh + 1) * T], ones_bf[:], osq[:, h, :], start=True, stop=True),
                         reads=["osq", "ones_bf"], writes=[BN(b)])
                P.op("act", lambda e, hp=hp, b=b: e.activation(ors[:, 2 * hp:2 * hp + 2, :].rearrange("p c t -> p (c t)"), banks[b][:], AF.Sqrt,
                                                               bias=eps_ap, scale=1.0 / 128), reads=[BN(b), "cst"], writes=["ors"])
            P.op("dve", lambda e: e.reciprocal(fl(ors), fl(ors)), reads=["ors"], writes=["ors"])
            for h in range(4):
                P.op("dve", lambda e, h=h: e.scalar_tensor_tensor(ytmp[:, h, :], osb[:, h, :], vec[:, V_HG + h:V_HG + h + 1], ors[:, h, :],
                                                                 ALU.mult, ALU.mult), reads=["osb", "ors", "vec"], writes=["ytmp"])
            P.op("pool", lambda e: e.tensor_tensor(fl(yb), fl(ytmp), fl(sg), ALU.mult), reads=["ytmp", "sg"], writes=["yb"])
            for ocp in range(4):
                b = bank()
                for oo in range(2):
                    oc = ocp * 2 + oo
                    for kc in range(8):
                        src = ya[:, kc, :] if kc < 4 else yb[:, kc - 4, :]
                        P.op("pe", lambda e, oc=oc, oo=oo, kc=kc, b=b, src=src: e.matmul(
                            banks[b][:, oo * T:(oo + 1) * T], w_oab_sb[:, kc, oc * 128:(oc + 1) * 128], src,
                            start=(kc == 0), stop=(kc == 7)), reads=["ya", "yb", "w_oab_sb"], writes=[BN(b)])
                for oo in range(2):
                    oc = ocp * 2 + oo
                    P.op("dve", lambda e, oc=oc, oo=oo, b=b: e.scalar_tensor_tensor(
                        xT[:, oc, :], banks[b][:, oo * T:(oo + 1) * T], gate1(0, oc), xT[:, oc, :], ALU.mult, ALU.add),
                        reads=[BN(b), "xT", "mod"], writes=["xT"])
            P.dma("sp", fm(x1T, i * T), xT[:], reads=["xT"], writes=["x1T_%d" % i], key="stA", final=dbg)
        P.barrier()

    if "A" in phases:
        _phA()

    def mlp_phase(l, src, src_pref, dst, dst_pref, tag, reverse_store, final_out, tiles):
        sb_off[0] = persist_end
        set_rot(list(range(8)))
        w1 = SB("w1", [128, 8, 4096], BF16)
        w2 = SB("w2", [128, 32, 1024], BF16)
        xTs = [SB("xTa", [128, 8, T]), SB("xTb", [128, 8, T])]
        sq = SB("sq", [128, 8, T], BF16)
        xn = SB("xn", [128, 8, T])
        rs = SB("rs", [128, T])
        hT = SB("hT", [128, 8, T], BF16)
        hid = SB("hid", [128, 32, T], BF16)
        rt = [SB("rt0", [128, 2 * T]), SB("rt1", [128, 2 * T])]
        xo = SB("xo", [128, 8, T]) if not final_out else SB("xo", [128, 2, 1024])
        load_w(w1, "w1", w1_in[l], 8, "w" + tag)
        load_w(w2, "w2", w2_in[l], 32, "w" + tag)

        def ld(i):
            s = i % 2
            P.dma("sp", xTs[s][:], fm(src, i * T), reads=["%s_%d" % (src_pref, i)], writes=["xT%d" % s], key="ld%s%d" % (tag, s))

        ld(tiles[0])
        for idx, i in enumerate(tiles):
            s = i % 2
            if idx + 1 < len(tiles):
                ld(tiles[idx + 1])
            xT = xTs[s]
            xname = "xT%d" % s
            rms_mod(xT, xname, hT, "hT", sq, xn, rs, A2, l, shift2)
            mlp_tile(l, hT, "hT", w1, w2, hid, rt, xT, xname)
            if final_out:
                P.op("dve", lambda e, xT=xT: e.tensor_copy(xn[:], xT[:, :, ::-1]), reads=[xname], writes=["xn"])
                for su in range(2):
                    for cq in range(2):
                        b = bank()
                        for c4 in range(4):
                            c = cq * 4 + c4
                            P.op("pe", lambda e, c=c, c4=c4, su=su, b=b: e.transpose(
                                banks[b][:, c4 * 128:(c4 + 1) * 128], xn[:, c, su * 128:(su + 1) * 128], ident),
                                reads=["xn", "cst"], writes=[BN(b)])
                        if cq == 0:
                            P.op("act", lambda e, su=su, cq=cq, b=b: e.copy(xo[:, su, cq * 512:(cq + 1) * 512], banks[b][:]),
                                 reads=[BN(b)], writes=["xo"])
                        else:
                            P.op("dve", lambda e, su=su, cq=cq, b=b: e.tensor_copy(xo[:, su, cq * 512:(cq + 1) * 512], banks[b][:]),
                                 reads=[BN(b)], writes=["xo"])
                for su in range(2):
                    r0 = S - (i + 1) * T + su * 128
                    P.dma("sp", dst[r0:r0 + 128, :], xo[:, su, :], reads=["xo"], key="stO", final=True)
            elif reverse_store:
                P.op("dve", lambda e, xT=xT: e.tensor_copy(xo[:], xT[:, :, ::-1]), reads=[xname], writes=["xo"])
                P.dma("sp", fm(dst, S - (i + 1) * T), xo[:], reads=["xo"], writes=["%s_%d" % (dst_pref, NT - 1 - i)], key="st" + tag, final=dbg)
            else:
                P.dma("sp", fm(dst, i * T), xT[:], reads=[xname], writes=["%s_%d" % (dst_pref, i)], key="st" + tag, final=dbg)
        P.barrier()

    if "B" in phases:
        mlp_phase(0, x1T, "x1T", x2T, "x2T", "B", True, False, tilesAB)

    def _phC():
        sb_off[0] = persist_end
        set_rot(list(range(8)))
        wq = SB("wq", [128, 8, 3072], BF16)
        xTs = [SB("xTa", [128, 8, T]), SB("xTb", [128, 8, T])]
        sq = SB("sq", [128, 8, T], BF16)
        xn = SB("xn", [128, 8, T])
        rs = SB("rs", [128, T])
        hT = SB("hT", [128, 8, T], BF16)
        sq2 = [SB("sq2a", [128, 2 * T], BF16), SB("sq2b", [128, 2 * T], BF16)]
        rr = [SB("rra", [128, 2 * T]), SB("rrb", [128, 2 * T])]
        qst = SB("qst", [128, 8, T], BF16)
        kst = SB("kst", [128, 8, T], BF16)
        vst = SB("vst", [128, 2, 1024], BF16)
        load_w(wq, "wq", w_qkv, 8, "wC")

        def ldc(i):
            s = i % 2
            P.dma("sp", xTs[s][:], fm(x2T, i * T), reads=["x2T_%d" % i], writes=["xT%d" % s], key="ldC%d" % s)

        ldc(tilesCDE[0])
        for idx, i in enumerate(tilesCDE):
            s = i % 2
            if idx + 1 < len(tilesCDE):
                ldc(tilesCDE[idx + 1])
            xT = xTs[s]
            xname = "xT%d" % s
            rms_mod(xT, xname, hT, "hT", sq, xn, rs, A1, 1, shift1)
            for j in range(8):
                b = bank()
                for qk in range(2):
                    for kc in range(8):
                        P.op("pe", lambda e, j=j, qk=qk, kc=kc, b=b: e.matmul(
                            banks[b][:, qk * T:(qk + 1) * T], wq[:, kc, qk * 1024 + j * 128: qk * 1024 + (j + 1) * 128], hT[:, kc, :],
                            start=(kc == 0), stop=(kc == 7)), reads=["hT", "wq"], writes=[BN(b)])
                a = j % 2
                P.op("act", lambda e, b=b, a=a: e.activation(sq2[a][:], banks[b][:], AF.Square), reads=[BN(b)], writes=["sq2%d" % a])
                b2 = bank()
                P.op("pe", lambda e, b2=b2, a=a: e.matmul(banks[b2][:], bd_bf[:], sq2[a][:], start=True, stop=True),
                     reads=["sq2%d" % a, "bd_bf"], writes=[BN(b2)])
                P.op("act", lambda e, b2=b2, a=a: e.activation(rr[a][:], banks[b2][:], AF.Sqrt, bias=eps_ap, scale=1.0 / 64),
                     reads=[BN(b2), "cst"], writes=["rr%d" % a])
                P.op("dve", lambda e, a=a: e.reciprocal(rr[a][:], rr[a][:]), reads=["rr%d" % a], writes=["rr%d" % a])
                P.op("dve", lambda e, b=b, a=a, j=j: e.scalar_tensor_tensor(qst[:, j, :], banks[b][:, 0:T], vec[:, V_QN:V_QN + 1], rr[a][:, 0:T],
                                                                           ALU.mult, ALU.mult), reads=[BN(b), "rr%d" % a, "vec"], writes=["qst"])
                P.op("dve", lambda e, b=b, a=a, j=j: e.scalar_tensor_tensor(kst[:, j, :], banks[b][:, T:2 * T], vec[:, V_KN:V_KN + 1], rr[a][:, T:2 * T],
                                                                           ALU.mult, ALU.mult), reads=[BN(b), "rr%d" % a, "vec"], writes=["kst"])
            for su in range(2):
                for hf in range(2):
                    b = bank()
                    for kc in range(8):
                        P.op("pe", lambda e, su=su, hf=hf, kc=kc, b=b: e.matmul(
                            banks[b][:], hT[:, kc, su * 128:(su + 1) * 128], wq[:, kc, 2048 + hf * 512: 2048 + (hf + 1) * 512],
                            start=(kc == 0), stop=(kc == 7)), reads=["hT", "wq"], writes=[BN(b)])
                    P.op("act", lambda e, su=su, hf=hf, b=b: e.copy(vst[:, su, hf * 512:(hf + 1) * 512], banks[b][:]),
                         reads=[BN(b)], writes=["vst"])
            P.dma("sp", fm(qTs, i * T), qst[:], reads=["qst"], writes=["qTs"], key="stC", final=dbg)
            P.dma("sp", fm(kTs, i * T), kst[:], reads=["kst"], writes=["kTs"], key="stC", final=dbg)
            P.dma("sp", vS[i * T:(i + 1) * T, :].rearrange("(s p) f -> p s f", p=128), vst[:], reads=["vst"], writes=["vS"], key="stC", final=dbg)
        P.barrier()

        sb_off[0] = persist_end
        set_rot([0, 1, 2, 3, 4])
        B_PO = [6, 7]
        B_OT = 5
        qTl = [SB("qTl0", [128, S], BF16), SB("qTl1", [128, S], BF16)]
        kTl = [SB("kTl0", [128, S], BF16), SB("kTl1", [128, S], BF16)]
        vl = [SB("vl0", [128, 32, 128], BF16), SB("vl1", [128, 32, 128], BF16)]
        NR = 3
        om = [SB("om%d" % r, [128, 512]) for r in range(NR)]
        Pb = [SB("Pb%d" % r, [128, 516]) for r in range(NR)]
        wn = [SB("wn%d" % r, [128, 512], BF16) for r in range(NR)]
        wT = [SB("wT%d" % r, [128, 512], BF16) for r in range(NR)]
        ost = [SB("ost0", [128, 128], BF16), SB("ost1", [128, 128], BF16)]
        oTl = [SB("oTl0", [128, S], BF16), SB("oTl1", [128, S], BF16)]

        if first_only:
            for s_ in range(2):
                P.op("pool", lambda e, s_=s_: e.memset(oTl[s_][:], 0.0), writes=["oTl%d" % s_])

        def ld2(j):
            s = j % 2
            P.dma("sp", qTl[s][:], qTs[j], reads=["qTs"], writes=["qTl%d" % s], key="ldQ%d" % s)
            P.dma("sp", kTl[s][:], kTs[j], reads=["kTs"], writes=["kTl%d" % s], key="ldQ%d" % s)
            P.dma("sp", vl[s][:], vS[:, j * 128:(j + 1) * 128].rearrange("(b p) f -> p b f", p=128), reads=["vS"], writes=["vl%d" % s],
                  key="ldQ%d" % s)

        ld2(0)
        ti = 0
        for j in range(8):
            s = j % 2
            if j + 1 < 8:
                ld2(j + 1)
            for qb in qbs:
                pi = qb % 2
                po = banks[B_PO[pi]]
                for hh in range(2):
                    pr = slice(64 * hh, 64 * hh + 64)
                    s0 = 128 * qb
                    ntile = (S - s0 + 511) // 512
                    for it in range(ntile):
                        k0 = s0 + 512 * it
                        W = min(512, S - k0)
                        r = ti % NR
                        rp = (ti - 1) % NR
                        ti += 1
                        b = bank()
                        P.op("pe", lambda e, b=b, pr=pr, s=s, s0=s0, k0=k0, W=W: e.matmul(
                            banks[b][:, 0:W], qTl[s][pr, s0:s0 + 128], kTl[s][pr, k0:k0 + W], start=True, stop=True),
                            reads=["qTl%d" % s, "kTl%d" % s], writes=[BN(b)])
                        P.op("act", lambda e, b=b, r=r, W=W: e.activation(om[r][:, 0:W], banks[b][:, 0:W], AF.Sigmoid, scale=-0.125),
                             reads=[BN(b)], writes=["om%d" % r])
                        if it == 0:
                            P.op("dve", lambda e, r=r: e.tensor_tensor(om[r][:, 0:128], om[r][:, 0:128], tril, ALU.max),
                                 reads=["om%d" % r, "cst"], writes=["om%d" % r])
                            P.op("pool", lambda e, r=r: e.memset(Pb[r][:, 0:1], 1.0), writes=["Pb%d" % r])
                        else:
                            Wp = 512
                            P.op("pool", lambda e, r=r, rp=rp, Wp=Wp: e.tensor_copy(Pb[r][:, 0:1], Pb[rp][:, Wp:Wp + 1]),
                                 reads=["Pb%d" % rp], writes=["Pb%d" % r])
                        P.op("dve", lambda e, r=r, W=W: e.tensor_tensor_scan(Pb[r][:, 1:1 + W], om[r][:, 0:W], onesf[:, 0:W], Pb[r][:, 0:1],
                                                                             ALU.mult, ALU.mult),
                             reads=["om%d" % r, "Pb%d" % r, "cst"], writes=["Pb%d" % r])
                        P.op("dve", lambda e, r=r, W=W: e.scalar_tensor_tensor(wn[r][:, 0:W], om[r][:, 0:W], 1.0, Pb[r][:, 0:W],
                                                                             ALU.subtract, ALU.mult),
                             reads=["om%d" % r, "Pb%d" % r], writes=["wn%d" % r])
                        b2 = bank()
                        nu = W // 128
                        for u in range(nu):
                            P.op("pe", lambda e, b2=b2, r=r, u=u: e.transpose(banks_bf[b2][:, u * 128:(u + 1) * 128], wn[r][:, u * 128:(u + 1) * 128], id_bf[:]),
                                 reads=["wn%d" % r, "id_bf"], writes=[BN(b2)])
                        P.op("act", lambda e, b2=b2, r=r, W=W: e.copy(wT[r][:, 0:W], banks_bf[b2][:, 0:W]), reads=[BN(b2)], writes=["wT%d" % r])
                        for u in range(nu):
                            blk = k0 // 128 + u
                            first = (it == 0 and u == 0)
                            last = (it == ntile - 1 and u == nu - 1)
                            P.op("pe", lambda e, r=r, u=u, blk=blk, first=first, last=last, hh=hh, s=s, po=po: e.matmul(
                                po[:, hh * 64:(hh + 1) * 64], wT[r][:, u * 128:(u + 1) * 128], vl[s][:, blk, hh * 64:(hh + 1) * 64],
                                start=first, stop=last, skip_group_check=True),
                                reads=["wT%d" % r, "vl%d" % s], writes=[BN(B_PO[pi])])
                P.op("act", lambda e, pi=pi, po=po: e.mul(ost[pi][:], po[:, 0:128], -1.0), reads=[BN(B_PO[pi])], writes=["ost%d" % pi])
                P.op("pe", lambda e, pi=pi: e.transpose(banks_bf[B_OT][:, 0:128], ost[pi][:], id_bf[:]), reads=["ost%d" % pi, "id_bf"],
                     writes=[BN(B_OT)])
                P.op("act", lambda e, s=s, qb=qb: e.copy(oTl[s][:, qb * 128:(qb + 1) * 128], banks_bf[B_OT][:, 0:128]), reads=[BN(B_OT)],
                     writes=["oTl%d" % s])
            P.dma("sp", oTs[j], oTl[s][:], reads=["oTl%d" % s], writes=["oTs"], key="stQ", final=dbg)
        P.barrier()

    if "C" in phases:
        _phC()

    def _phD():
        sb_off[0] = persist_end
        set_rot(list(range(8)))
        woc = SB("woc", [128, 8, 1024], BF16)
        xTs = [SB("xTa", [128, 8, T]), SB("xTb", [128, 8, T])]
        oTt = [SB("oTa", [128, 8, T], BF16), SB("oTb", [128, 8, T], BF16)]
        load_w(woc, "woc", w_oc, 8, "wD")

        def ldd(i):
            s = i % 2
            P.dma("sp", xTs[s][:], fm(x2T, i * T), reads=["x2T_%d" % i], writes=["xT%d" % s], key="ldD%d" % s)
            P.dma("sp", oTt[s][:], fm(oTs, i * T), reads=["oTs"], writes=["oT%d" % s], key="ldD%d" % s)

        ldd(tilesCDE[0])
        for idx, i in enumerate(tilesCDE):
            s = i % 2
            if idx + 1 < len(tilesCDE):
                ldd(tilesCDE[idx + 1])
            xT = xTs[s]
            for ocp in range(4):
                b = bank()
                for oo in range(2):
                    oc = ocp * 2 + oo
                    for kc in range(8):
                        P.op("pe", lambda e, oc=oc, oo=oo, kc=kc, b=b, s=s: e.matmul(
                            banks[b][:, oo * T:(oo + 1) * T], woc[:, kc, oc * 128:(oc + 1) * 128], oTt[s][:, kc, :],
                            start=(kc == 0), stop=(kc == 7)), reads=["oT%d" % s, "woc"], writes=[BN(b)])
                for oo in range(2):
                    oc = ocp * 2 + oo
                    P.op("dve", lambda e, oc=oc, oo=oo, b=b, xT=xT: e.scalar_tensor_tensor(
                        xT[:, oc, :], banks[b][:, oo * T:(oo + 1) * T], gate1(1, oc), xT[:, oc, :], ALU.mult, ALU.add),
                        reads=[BN(b), "xT%d" % s, "mod"], writes=["xT%d" % s])
            P.dma("sp", fm(x3T, i * T), xT[:], reads=["xT%d" % s], writes=["x3T_%d" % i], key="stD", final=dbg)
        P.barrier()

    if "D" in phases:
        _phD()

    if "E" in phases:
        mlp_phase(1, x3T, "x3T", out, "out", "E", False, True, tilesCDE)

    P.emit()
    return nc


_CACHE = {}


def kernel(x, c, ada_w, ada_b, norm_mix, norm_mlp, w_in_ab, conv_w, hg_norm, lb_logits, w_out_ab,
           w_qkv, q_norm, k_norm, w_out_c, mlp_w1, mlp_w2):
    f = lambda a: np.ascontiguousarray(np.asarray(a, dtype=np.float32))
    x = f(x)
    c = f(c)
    if "nc" not in _CACHE:
        _CACHE["nc"] = build()
    nc = _CACHE["nc"]
    consts = make_consts()
    vecs = make_vecs(f(ada_b), f(norm_mix), f(norm_mlp), f(conv_w), f(hg_norm), f(lb_logits), f(q_norm), f(k_norm))
    shared = {
        "constf": consts, "vecs": vecs, "ada_w": f(ada_w), "w_in_ab": f(w_in_ab)[0], "w_out_ab": f(w_out_ab)[0],
        "w_qkv": f(w_qkv)[0], "w_out_c": f(w_out_c)[0], "mlp_w1": f(mlp_w1), "mlp_w2": f(mlp_w2),
    }
    in_maps = []
    for b in range(8):
        m = dict(shared)
        m["x"] = x[b]
        m["cfm"] = np.ascontiguousarray(c[b].reshape(8, 128).T)
        in_maps.append(m)
    res = run_bass_kernel_spmd(nc, in_maps, core_ids=list(range(8)))
    return np.stack([np.asarray(r["out"], dtype=np.float32) for r in res.results], axis=0)
```

```python
import contextlib
import numpy as np
import ml_dtypes
import concourse.bass as bass
import concourse.mybir as mybir
from concourse.bass_utils import run_bass_kernel_spmd

F32 = mybir.dt.float32
BF16 = mybir.dt.bfloat16
AF = mybir.ActivationFunctionType
ALU = mybir.AluOpType

COMPUTE = ("pe", "act", "dve", "pool")


class Op:
    __slots__ = ("eng", "fn", "deps", "signal", "sigval", "is_dma", "semkey", "dmaval")

    def __init__(self, eng, fn, is_dma=False, semkey=None):
        self.eng = eng
        self.fn = fn
        self.deps = []
        self.signal = False
        self.sigval = 0
        self.is_dma = is_dma
        self.semkey = semkey
        self.dmaval = 0


class Prog:
    def __init__(self, nc):
        self.nc = nc
        self.ops = {e: [] for e in ("pe", "act", "dve", "pool", "sp")}
        self.last_w = {}
        self.readers = {}
        self.dma_counts = {}
        self.final_dma = []
        self.bar = None
        self.bar_done = set()
        self.phase_dmas = []

    def barrier(self):
        deps = []
        for e in COMPUTE:
            if self.ops[e]:
                for o in reversed(self.ops[e]):
                    if not o.is_dma:
                        deps.append(o)
                        break
        deps.extend(self.phase_dmas)
        self.phase_dmas = []
        self.bar = deps
        self.bar_done = set()

    def _hazards(self, op, reads, writes):
        deps = []
        if self.bar is not None and op.eng not in self.bar_done:
            self.bar_done.add(op.eng)
            deps.extend(self.bar)
        for b in reads:
            w = self.last_w.get(b)
            if w is not None:
                deps.append(w)
        for b in writes:
            w = self.last_w.get(b)
            if w is not None:
                deps.append(w)
            deps.extend(self.readers.get(b, ()))
        seen = set()
        out = []
        dkeys = {}
        for d in deps:
            if id(d) in seen or d is op:
                continue
            seen.add(id(d))
            if d.is_dma:
                dkeys[d.semkey] = 16 * self.dma_counts[d.semkey]
                continue
            if d.eng == "pe" and op.eng == "pe" and not op.is_dma:
                continue
            out.append(d)
        for k, v in dkeys.items():
            out.append(("dma", k, v))
        op.deps = out
        for d in out:
            if not isinstance(d, tuple):
                d.signal = True
        for b in reads:
            lst = self.readers.setdefault(b, [])
            if not op.is_dma:
                for i, r in enumerate(lst):
                    if (not r.is_dma) and r.eng == op.eng:
                        lst[i] = op
                        break
                else:
                    lst.append(op)
            else:
                lst.append(op)
        for b in writes:
            self.last_w[b] = op
            self.readers[b] = []

    def op(self, eng, fn, reads=(), writes=()):
        o = Op(eng, fn)
        self._hazards(o, reads, writes)
        self.ops[eng].append(o)
        return o

    def dma(self, q, out, in_, reads=(), writes=(), key=None, final=False):
        o = Op(q, (out, in_), is_dma=True, semkey=key)
        self._hazards(o, reads, writes)
        self.dma_counts[key] = self.dma_counts.get(key, 0) + 1
        o.dmaval = 16 * self.dma_counts[key]
        self.ops[q].append(o)
        self.phase_dmas.append(o)
        if final:
            self.final_dma.append(o)
        return o

    def emit(self):
        nc = self.nc
        with contextlib.ExitStack() as es:
            sems = {}
            for e in COMPUTE:
                sems[e] = es.enter_context(nc.semaphore("s_" + e))
            dsem = {}
            for k in self.dma_counts:
                dsem[k] = es.enter_context(nc.semaphore("d_%s" % (k,)))
            for e in COMPUTE:
                c = 0
                for o in self.ops[e]:
                    if o.signal:
                        c += 1
                        o.sigval = c
            block = es.enter_context(nc.Block())
            engmap = {"pe": "tensor", "act": "scalar", "dve": "vector", "pool": "gpsimd", "sp": "sync"}

            def run(ename, eobj):
                waited = {}
                for o in self.ops[ename]:
                    for d in o.deps:
                        if isinstance(d, tuple):
                            k = ("dma", d[1])
                            v = d[2]
                            s = dsem[d[1]]
                        else:
                            k = d.eng
                            v = d.sigval
                            s = sems[d.eng]
                        if waited.get(k, 0) >= v:
                            continue
                        waited[k] = v
                        eobj.wait_ge(s, v)
                    if o.is_dma:
                        out, in_ = o.fn
                        eobj.dma_start(out=out, in_=in_).then_inc(dsem[o.semkey], 16)
                    else:
                        ins = o.fn(eobj)
                        if o.signal:
                            ins.then_inc(sems[ename], 1)
                if ename == "sp":
                    for kk in sorted({o.semkey for o in self.final_dma}):
                        v = 16 * self.dma_counts[kk]
                        if waited.get(("dma", kk), 0) < v:
                            waited[("dma", kk)] = v
                            eobj.wait_ge(dsem[kk], v)

            for ename in ("pe", "act", "dve", "pool", "sp"):
                deco = getattr(block, engmap[ename])

                def body(eobj, ename=ename):
                    run(ename, eobj)
                deco(body)


D = 1024
S = 4096
T = 256
NT = S // T
EPS = 1e-6

C_ID = 0
C_J = 128
C_TRIL = 256
C_M64 = 384
C_SCAN = 448
C_ONES = C_SCAN + 1024
C_BD = C_ONES + 512
C_EPS = C_BD + 128
NCF = C_EPS + 1
V_B = 0
V_NM = 96
V_NL = 112
V_CW = 128
V_HG = 140
V_LB = 144
V_QN = 156
V_KN = 157
NV = 158


def make_consts():
    c = np.zeros((128, NCF), np.float32)
    c[:, C_ID:C_ID + 128] = np.eye(128, dtype=np.float32)
    c[:, C_J:C_J + 128] = np.eye(128, dtype=np.float32)[::-1]
    p = np.arange(128)
    c[:, C_TRIL:C_TRIL + 128] = (p[None, :] <= p[:, None]).astype(np.float32)
    s64 = np.arange(64)
    c[:64, C_M64:C_M64 + 64] = (s64[:, None] <= s64[None, :]).astype(np.float32)
    m = np.ones(1024, np.float32)
    m[::64] = 0.0
    c[:, C_SCAN:C_SCAN + 1024] = m[None, :]
    c[:, C_ONES:C_ONES + 512] = 1.0
    bd = np.zeros((128, 128), np.float32)
    bd[:64, :64] = 1.0
    bd[64:, 64:] = 1.0
    c[:, C_BD:C_BD + 128] = bd
    c[:, C_EPS] = EPS
    return c


def make_vecs(ada_b, norm_mix, norm_mlp, conv_w, hg_norm, lb_logits, q_norm, k_norm):
    v = np.zeros((128, NV), np.float32)
    v[:, V_B:V_B + 96] = ada_b.reshape(2, 48, 128).transpose(2, 0, 1).reshape(128, 96)
    v[:, V_NM:V_NM + 16] = norm_mix.reshape(2, 8, 128).transpose(2, 0, 1).reshape(128, 16)
    v[:, V_NL:V_NL + 16] = norm_mlp.reshape(2, 8, 128).transpose(2, 0, 1).reshape(128, 16)
    v[:, V_CW:V_CW + 12] = conv_w.reshape(3, 4, 128).transpose(2, 0, 1).reshape(128, 12)
    v[:, V_HG:V_HG + 4] = hg_norm.reshape(4, 128).T
    v[:, V_LB:V_LB + 12] = lb_logits.reshape(3, 4, 128).transpose(2, 0, 1).reshape(128, 12)
    v[:, V_QN] = np.tile(q_norm.reshape(64), 2)
    v[:, V_KN] = np.tile(k_norm.reshape(64), 2)
    return v


def build(phases="MABCDE", dbg=False, first_only=False):
    nc = bass.Bass("TRN2", target_bir_lowering=False)

    def dram(name, shape, dt, kind):
        return nc.dram_tensor(name, shape, dt, kind=kind).ap()

    x_in = dram("x", [S, D], F32, "ExternalInput")
    c_in = dram("cfm", [128, 8], F32, "ExternalInput")
    constf = dram("constf", [128, NCF], F32, "ExternalInput")
    vecs_in = dram("vecs", [128, NV], F32, "ExternalInput")
    ada_w = dram("ada_w", [2, 1024, 6144], F32, "ExternalInput")
    w_in = dram("w_in_ab", [1024, 3584], F32, "ExternalInput")
    w_oab = dram("w_out_ab", [1024, 1024], F32, "ExternalInput")
    w_qkv = dram("w_qkv", [1024, 3072], F32, "ExternalInput")
    w_oc = dram("w_out_c", [1024, 1024], F32, "ExternalInput")
    w1_in = dram("mlp_w1", [2, 1024, 4096], F32, "ExternalInput")
    w2_in = dram("mlp_w2", [2, 4096, 1024], F32, "ExternalInput")
    out = dram("out", [S, D], F32, "ExternalOutput")
    sk = "ExternalOutput" if dbg else "Internal"
    x1T = dram("x1T", [8, 128, S], F32, sk)
    x2T = dram("x2T", [8, 128, S], F32, sk)
    x3T = dram("x3T", [8, 128, S], F32, sk)
    qTs = dram("qTs", [8, 128, S], BF16, sk)
    kTs = dram("kTs", [8, 128, S], BF16, sk)
    vS = dram("vS", [S, 1024], BF16, sk)
    oTs = dram("oTs", [8, 128, S], BF16, sk)
    mod_dbg = dram("mod_dbg", [128, 96], F32, "ExternalOutput") if dbg else None

    nfo = int(first_only)
    tilesAB = list(range(nfo)) if first_only else list(range(NT))
    tilesCDE = list(range(NT - nfo, NT)) if first_only else list(range(NT))
    qbs = list(range(32 - 2 * nfo, 32)) if first_only else list(range(32))

    P = Prog(nc)
    SB_BASE = 16512
    SB_LIMIT = 229300
    sb_off = [SB_BASE]
    uid = [0]

    def SB(name, shape, dt=F32):
        nbytes = int(np.prod(shape[1:])) * (4 if dt == F32 else 2)
        off = (sb_off[0] + 31) // 32 * 32
        sb_off[0] = off + nbytes
        assert sb_off[0] <= SB_LIMIT, (name, sb_off[0])
        uid[0] += 1
        return nc.alloc_sbuf_tensor_at("%s_%d" % (name, uid[0]), shape, dt, offset=off)

    banks = [nc.alloc_psum_tensor("pb%d" % i, [128, 512], F32) for i in range(8)]
    banks_bf = [b.bitcast(BF16) for b in banks]
    rot = {"lst": list(range(8)), "i": 0}

    def set_rot(lst):
        rot["lst"] = list(lst)
        rot["i"] = 0

    def bank():
        b = rot["lst"][rot["i"] % len(rot["lst"])]
        rot["i"] += 1
        return b

    def BN(b):
        return "pb%d" % b

    cst = SB("cst", [128, NCF])
    vec = SB("vec", [128, NV])
    csb = SB("csb", [128, 8])
    cact = SB("cact", [128, 8])
    mod = SB("mod", [128, 96])
    A1 = SB("A1", [128, 16])
    A2 = SB("A2", [128, 16])
    lbv = SB("lbv", [128, 4])
    omlb = SB("omlb", [128, 4])
    lbt = SB("lbt", [128, 12])
    lbs = SB("lbs", [128, 4])
    id_bf = SB("id_bf", [128, 128], BF16)
    ones_bf = SB("ones_bf", [128, 128], BF16)
    bd_bf = SB("bd_bf", [128, 128], BF16)
    zero_bf = SB("zero_bf", [128, 128], BF16)
    persist_end = sb_off[0]

    ident = cst[:, C_ID:C_ID + 128]
    jmat = cst[:, C_J:C_J + 128]
    tril = cst[:, C_TRIL:C_TRIL + 128]
    onesf = cst[:, C_ONES:C_ONES + 512]
    eps_ap = cst[:, C_EPS:C_EPS + 1]

    P.dma("sp", cst[:], constf, writes=["cst"], key="c0")
    P.dma("sp", vec[:], vecs_in, writes=["vec"], key="c0")
    P.dma("sp", csb[:], c_in, writes=["csb"], key="c0")
    P.op("dve", lambda e: e.tensor_copy(id_bf[:], ident), reads=["cst"], writes=["id_bf"])
    P.op("dve", lambda e: e.tensor_copy(ones_bf[:], cst[:, C_ONES:C_ONES + 128]), reads=["cst"], writes=["ones_bf"])
    P.op("dve", lambda e: e.tensor_copy(bd_bf[:], cst[:, C_BD:C_BD + 128]), reads=["cst"], writes=["bd_bf"])
    P.op("pool", lambda e: e.memset(zero_bf[:], 0.0), writes=["zero_bf"])

    def shift1(l, c):
        return mod[:, l * 48 + 0 + c: l * 48 + 0 + c + 1]

    def gate1(l, c):
        return mod[:, l * 48 + 16 + c: l * 48 + 16 + c + 1]

    def shift2(l, c):
        return mod[:, l * 48 + 24 + c: l * 48 + 24 + c + 1]

    def gate2(l, c):
        return mod[:, l * 48 + 40 + c: l * 48 + 40 + c + 1]

    def _phM():
        sb_off[0] = persist_end
        awb = [SB("awb0", [128, 6144]), SB("awb1", [128, 6144])]
        P.op("act", lambda e: e.activation(cact[:], csb[:], AF.Silu), reads=["csb"], writes=["cact"])
        pm = banks[0]
        P.op("pe", lambda e: e.matmul(pm[:, 0:96], zero_bf[:], zero_bf[:, 0:96], start=True, stop=True),
             reads=["zero_bf"], writes=["pb0"])
        for l in range(2):
            for kc in range(8):
                bi = kc % 2
                P.dma("sp", awb[bi][:], ada_w[l, kc * 128:(kc + 1) * 128, :], writes=["awb%d" % bi], key="aw%d" % bi)
                for j in range(48):
                    P.op("pe", lambda e, l=l, kc=kc, j=j, bi=bi: e.matmul(
                        pm[:, l * 48 + j: l * 48 + j + 1], awb[bi][:, j * 128:(j + 1) * 128], cact[:, kc:kc + 1],
                        start=False, stop=(l == 1 and kc == 7 and j == 47), skip_group_check=True),
                        reads=["awb%d" % bi, "cact"], writes=["pb0"])
        P.op("dve", lambda e: e.tensor_tensor(mod[:], pm[:, 0:96], vec[:, V_B:V_B + 96], ALU.add),
             reads=["pb0", "vec"], writes=["mod"])
        for l in range(2):
            P.op("dve", lambda e, l=l: e.scalar_tensor_tensor(
                A1[:, l * 8:(l + 1) * 8], mod[:, l * 48 + 8: l * 48 + 16], 1.0, vec[:, V_NM + l * 8: V_NM + (l + 1) * 8],
                ALU.add, ALU.mult), reads=["mod", "vec"], writes=["A1"])
            P.op("dve", lambda e, l=l: e.scalar_tensor_tensor(
                A2[:, l * 8:(l + 1) * 8], mod[:, l * 48 + 32: l * 48 + 40], 1.0, vec[:, V_NL + l * 8: V_NL + (l + 1) * 8],
                ALU.add, ALU.mult), reads=["mod", "vec"], writes=["A2"])
        P.op("act", lambda e: e.activation(lbt[:], vec[:, V_LB:V_LB + 12], AF.Exp), reads=["vec"], writes=["lbt"])
        P.op("dve", lambda e: e.tensor_tensor(lbs[:], lbt[:, 0:4], lbt[:, 4:8], ALU.add), reads=["lbt"], writes=["lbs"])
        P.op("dve", lambda e: e.tensor_tensor(lbs[:], lbs[:], lbt[:, 8:12], ALU.add), reads=["lbt", "lbs"], writes=["lbs"])
        P.op("dve", lambda e: e.reciprocal(lbs[:], lbs[:]), reads=["lbs"], writes=["lbs"])
        P.op("dve", lambda e: e.tensor_tensor(lbv[:], lbt[:, 0:4], lbs[:], ALU.mult), reads=["lbt", "lbs"], writes=["lbv"])
        P.op("dve", lambda e: e.tensor_scalar(omlb[:], lbv[:], -1.0, 1.0, ALU.mult, ALU.add), reads=["lbv"], writes=["omlb"])
        if dbg:
            P.dma("sp", mod_dbg, mod[:], reads=["mod"], key="dbg", final=True)
        P.barrier()

    if "M" in phases:
        _phM()

    def load_w(dst, dst_name, src, kcn, key):
        for kc in range(kcn):
            P.dma("pool", dst[:, kc, :], src[kc * 128:(kc + 1) * 128, :], writes=[dst_name], key=key)

    def rms_mod(xT, xT_name, hT, hT_name, sq, xn, rs, Avec, l, shiftf):
        P.op("act", lambda e: e.activation(sq[:].rearrange("p c t -> p (c t)"), xT[:].rearrange("p c t -> p (c t)"), AF.Square),
             reads=[xT_name], writes=["sq"])
        b = bank()
        for c in range(8):
            P.op("pe", lambda e, c=c, b=b: e.matmul(banks[b][:, 0:T], ones_bf[:], sq[:, c, :], start=(c == 0), stop=(c == 7)),
                 reads=["sq", "ones_bf"], writes=[BN(b)])
        P.op("act", lambda e, b=b: e.activation(rs[:], banks[b][:, 0:T], AF.Sqrt, bias=eps_ap, scale=1.0 / D),
             reads=[BN(b), "cst"], writes=["rs"])
        P.op("dve", lambda e: e.reciprocal(rs[:], rs[:]), reads=["rs"], writes=["rs"])
        rs_b = bass.AP(rs, 0, [[T, 128], [0, 8], [1, T]])
        P.op("dve", lambda e: e.tensor_tensor(xn[:], xT[:], rs_b, ALU.mult), reads=[xT_name, "rs"], writes=["xn"])
        for c in range(8):
            if c % 2 == 0:
                P.op("pool", lambda e, c=c: e.tensor_scalar(hT[:, c, :], xn[:, c, :], Avec[:, l * 8 + c: l * 8 + c + 1], shiftf(l, c),
                                                           ALU.mult, ALU.add), reads=["xn", "mod", "A1", "A2"], writes=[hT_name])
            else:
                P.op("act", lambda e, c=c: e.activation(hT[:, c, :], xn[:, c, :], AF.Identity, bias=shiftf(l, c),
                                                        scale=Avec[:, l * 8 + c: l * 8 + c + 1]),
                     reads=["xn", "mod", "A1", "A2"], writes=[hT_name])

    def mlp_tile(l, hT, hT_name, w1, w2, hid, rt, xT, xT_name):
        for jp in range(16):
            b = bank()
            for jj in range(2):
                j = jp * 2 + jj
                for kc in range(8):
                    P.op("pe", lambda e, j=j, jj=jj, kc=kc, b=b: e.matmul(
                        banks[b][:, jj * T:(jj + 1) * T], w1[:, kc, j * 128:(j + 1) * 128], hT[:, kc, :],
                        start=(kc == 0), stop=(kc == 7)), reads=[hT_name, "w1"], writes=[BN(b)])
            ri = jp % 2
            P.op("act", lambda e, b=b, ri=ri: e.activation(rt[ri][:], banks[b][:], AF.Relu), reads=[BN(b)], writes=["rt%d" % ri])
            eng = "pool" if jp % 2 == 0 else "dve"
            P.op(eng, lambda e, jp=jp, ri=ri: e.tensor_tensor(
                hid[:, 2 * jp:2 * jp + 2, :].rearrange("p c t -> p (c t)"), rt[ri][:], rt[ri][:], ALU.mult),
                reads=["rt%d" % ri], writes=["hid"])
        for ocp in range(4):
            b = bank()
            for oo in range(2):
                oc = ocp * 2 + oo
                for j in range(32):
                    P.op("pe", lambda e, oc=oc, oo=oo, j=j, b=b: e.matmul(
                        banks[b][:, oo * T:(oo + 1) * T], w2[:, j, oc * 128:(oc + 1) * 128], hid[:, j, :],
                        start=(j == 0), stop=(j == 31)), reads=["hid", "w2"], writes=[BN(b)])
            for oo in range(2):
                oc = ocp * 2 + oo
                P.op("dve", lambda e, oc=oc, oo=oo, b=b: e.scalar_tensor_tensor(
                    xT[:, oc, :], banks[b][:, oo * T:(oo + 1) * T], gate2(l, oc), xT[:, oc, :], ALU.mult, ALU.add),
                    reads=[BN(b), xT_name, "mod"], writes=[xT_name])

    def fm(dr, t0):
        return dr[:, :, t0:t0 + T].rearrange("c p t -> p c t")

    def _phA():
        sb_off[0] = persist_end
        set_rot([0, 1, 2, 3])
        B_O, B_SC, B_KT, B_DS = 7, 6, 5, 4
        w_in_sb = SB("w_in_sb", [128, 8, 3584], BF16)
        w_oab_sb = SB("w_oab_sb", [128, 8, 1024], BF16)
        xtm = [SB("xtm0", [128, 2, 1024]), SB("xtm1", [128, 2, 1024])]
        xT = SB("xT", [128, 8, T])
        sq = SB("sq", [128, 8, T], BF16)
        xn = SB("xn", [128, 8, T])
        rs = SB("rs", [128, T])
        hT = SB("hT", [128, 8, T], BF16)
        ah = [SB("ah0", [128, T]), SB("ah1", [128, T])]
        pbuf = SB("pbuf", [128, 4, T + 2])
        ct = [SB("ct0", [128, T]), SB("ct1", [128, T])]
        ya = SB("ya", [128, 4, T], BF16)
        fb = SB("fb", [128, 4, T])
        lf = SB("lf", [128, 4, T])
        bb = SB("bb", [128, 4, T])
        eb = SB("eb", [128, 4, T])
        kin = SB("kin", [128, 4, T])
        qe = SB("qe", [128, 4, T], BF16)
        ke = SB("ke", [128, 4, T], BF16)
        kd = SB("kd", [128, 4, T], BF16)
        vtm = SB("vtm", [64, 4, 512], BF16)
        Sst = SB("Sst", [128, 4, 128])
        Sbf = SB("Sbf", [128, 4, 128], BF16)
        scm = SB("scm", [64, 4, 64], BF16)
        kdT = SB("kdT", [64, 4, 128], BF16)
        osb = SB("osb", [128, 4, T])
        osq = SB("osq", [128, 4, T], BF16)
        sg = SB("sg", [128, 4, T])
        ors = SB("ors", [128, 4, T])
        ytmp = SB("ytmp", [128, 4, T])
        yb = SB("yb", [128, 4, T], BF16)

        load_w(w_in_sb, "w_in_sb", w_in, 8, "wA")
        load_w(w_oab_sb, "w_oab_sb", w_oab, 8, "wA")
        P.op("pool", lambda e: e.memset(pbuf[:], 0.0), writes=["pbuf"])
        P.op("pool", lambda e: e.memset(Sst[:], 0.0), writes=["Sst"])
        P.op("pool", lambda e: e.memset(Sbf[:], 0.0), writes=["Sbf"])

        def cw(w, j):
            return vec[:, V_CW + w * 4 + j: V_CW + w * 4 + j + 1]

        def load_x(i):
            s = i % 2
            P.dma("sp", xtm[s][:], x_in[i * T:(i + 1) * T, :].rearrange("(s p) f -> p s f", p=128),
                  writes=["xtm%d" % s], key="xtm%d" % s)

        def proj(col0, b, half):
            for kc in range(8):
                P.op("pe", lambda e, kc=kc: e.matmul(banks[b][:, half * T:(half + 1) * T], w_in_sb[:, kc, col0:col0 + 128], hT[:, kc, :],
                                                    start=(kc == 0), stop=(kc == 7)), reads=["hT", "w_in_sb"], writes=[BN(b)])

        load_x(tilesAB[0])
        for idx, i in enumerate(tilesAB):
            s = i % 2
            if idx + 1 < len(tilesAB):
                load_x(tilesAB[idx + 1])
            for cp in range(4):
                b = bank()
                for cc in range(2):
                    c = cp * 2 + cc
                    for su in range(2):
                        P.op("pe", lambda e, c=c, cc=cc, su=su, b=b, s=s: e.transpose(
                            banks[b][:, cc * T + su * 128: cc * T + (su + 1) * 128], xtm[s][:, su, c * 128:(c + 1) * 128], ident),
                            reads=["xtm%d" % s, "cst"], writes=[BN(b)])
                eng = "act" if cp % 2 == 0 else "dve"
                if eng == "act":
                    P.op("act", lambda e, cp=cp, b=b: e.copy(xT[:, 2 * cp:2 * cp + 2, :].rearrange("p c t -> p (c t)"), banks[b][:]),
                         reads=[BN(b)], writes=["xT"])
                else:
                    P.op("dve", lambda e, cp=cp, b=b: e.tensor_copy(xT[:, 2 * cp:2 * cp + 2, :].rearrange("p c t -> p (c t)"), banks[b][:]),
                         reads=[BN(b)], writes=["xT"])
            rms_mod(xT, "xT", hT, "hT", sq, xn, rs, A1, 0, shift1)
            P.op("pool", lambda e: e.tensor_copy(pbuf[:, :, 0:2], pbuf[:, :, T:T + 2]), reads=["pbuf"], writes=["pbuf"])
            for j in range(4):
                b = bank()
                proj(512 + j * 128, b, 0)
                proj(1024 + j * 128, b, 1)
                a = j % 2
                P.op("act", lambda e, b=b, a=a: e.copy(ah[a][:], banks[b][:, T:2 * T]), reads=[BN(b)], writes=["ah%d" % a])
                P.op("dve", lambda e, b=b, a=a, j=j: e.tensor_tensor(pbuf[:, j, 2:T + 2], banks[b][:, 0:T], ah[a][:], ALU.mult),
                     reads=[BN(b), "ah%d" % a], writes=["pbuf"])
                P.op("pool", lambda e, j=j, a=a: e.tensor_scalar(ct[a][:], pbuf[:, j, 0:T], cw(0, j), None, ALU.mult),
                     reads=["pbuf", "vec"], writes=["ct%d" % a])
                P.op("dve", lambda e, j=j, a=a: e.scalar_tensor_tensor(ct[a][:], pbuf[:, j, 1:T + 1], cw(1, j), ct[a][:], ALU.mult, ALU.add),
                     reads=["pbuf", "vec", "ct%d" % a], writes=["ct%d" % a])
                P.op("dve", lambda e, j=j, a=a: e.scalar_tensor_tensor(ct[a][:], pbuf[:, j, 2:T + 2], cw(2, j), ct[a][:], ALU.mult, ALU.add),
                     reads=["pbuf", "vec", "ct%d" % a], writes=["ct%d" % a])
                b2 = bank()
                proj(j * 128, b2, 0)
                P.op("dve", lambda e, j=j, a=a, b2=b2: e.tensor_tensor(ya[:, j, :], banks[b2][:, 0:T], ct[a][:], ALU.mult),
                     reads=[BN(b2), "ct%d" % a], writes=["ya"])
            for hp in range(2):
                b = bank()
                proj(2048 + (2 * hp) * 128, b, 0)
                proj(2048 + (2 * hp + 1) * 128, b, 1)
                P.op("act", lambda e, b=b, hp=hp: e.activation(fb[:, 2 * hp:2 * hp + 2, :].rearrange("p c t -> p (c t)"), banks[b][:], AF.Sigmoid),
                     reads=[BN(b)], writes=["fb"])
            for h in range(4):
                P.op("pool", lambda e, h=h: e.tensor_scalar(fb[:, h, :], fb[:, h, :], omlb[:, h:h + 1], lbv[:, h:h + 1], ALU.mult, ALU.add),
                     reads=["fb", "lbv", "omlb"], writes=["fb"])
            fl = lambda t: t[:].rearrange("p c t -> p (c t)")
            P.op("act", lambda e: e.activation(fl(lf), fl(fb), AF.Ln), reads=["fb"], writes=["lf"])
            P.op("dve", lambda e: e.tensor_tensor_scan(fl(bb), cst[:, C_SCAN:C_SCAN + 4 * T], fl(lf), 0.0, ALU.mult, ALU.add),
                 reads=["lf", "cst"], writes=["bb"])
            P.op("pool", lambda e: e.tensor_scalar(fl(kin), fl(fb), -1.0, 1.0, ALU.mult, ALU.add), reads=["fb"], writes=["kin"])
            P.op("act", lambda e: e.activation(fl(eb), fl(bb), AF.Exp), reads=["bb"], writes=["eb"])
            P.op("act", lambda e: e.activation(fl(lf), fl(bb), AF.Exp, scale=-1.0), reads=["bb"], writes=["lf"])
            P.op("pool", lambda e: e.tensor_tensor(fl(kin), fl(kin), fl(lf), ALU.mult), reads=["kin", "lf"], writes=["kin"])
            P.op("pool", lambda e: e.tensor_copy(fl(ke), fl(kin)), reads=["kin"], writes=["ke"])
            for h in range(4):
                ebl = bass.AP(eb, h * T + 63, [[4 * T, 128], [64, T // 64], [0, 64]])
                P.op("dve", lambda e, h=h, ebl=ebl: e.tensor_tensor(
                    kd[:, h, :].rearrange("p (c t) -> p c t", t=64), kin[:, h, :].rearrange("p (c t) -> p c t", t=64), ebl, ALU.mult),
                    reads=["kin", "eb"], writes=["kd"])
            for hp in range(2):
                b = bank()
                proj(1536 + (2 * hp) * 128, b, 0)
                proj(1536 + (2 * hp + 1) * 128, b, 1)
                P.op("dve", lambda e, b=b, hp=hp: e.tensor_tensor(
                    qe[:, 2 * hp:2 * hp + 2, :].rearrange("p c t -> p (c t)"), banks[b][:],
                    eb[:, 2 * hp:2 * hp + 2, :].rearrange("p c t -> p (c t)"), ALU.mult), reads=[BN(b), "eb"], writes=["qe"])
            for hp in range(2):
                b = bank()
                proj(3072 + (2 * hp) * 128, b, 0)
                proj(3072 + (2 * hp + 1) * 128, b, 1)
                P.op("act", lambda e, b=b, hp=hp: e.activation(sg[:, 2 * hp:2 * hp + 2, :].rearrange("p c t -> p (c t)"), banks[b][:], AF.Silu),
                     reads=[BN(b)], writes=["sg"])
            for cc in range(4):
                b = bank()
                for kc in range(8):
                    P.op("pe", lambda e, kc=kc, cc=cc, b=b: e.matmul(banks[b][0:64, :], hT[:, kc, cc * 64:(cc + 1) * 64], w_in_sb[:, kc, 2560:3072],
                                                                   start=(kc == 0), stop=(kc == 7)), reads=["hT", "w_in_sb"], writes=[BN(b)])
                P.op("act", lambda e, cc=cc, b=b: e.copy(vtm[:, cc, :], banks[b][0:64, :]), reads=[BN(b)], writes=["vtm"])
            pO, pSC, pKT, pDS = banks[B_O], banks[B_SC], banks_bf[B_KT], banks[B_DS]
            m64 = bass.AP(cst, C_M64, [[NCF, 64], [0, 4], [1, 64]])
            for cc in range(4):
                tsl = slice(cc * 64, (cc + 1) * 64)
                for h in range(4):
                    P.op("pe", lambda e, h=h, tsl=tsl: e.matmul(pSC[0:64, h * 64:(h + 1) * 64], ke[:, h, tsl], qe[:, h, tsl], start=True, stop=True),
                         reads=["ke", "qe"], writes=[BN(B_SC)])
                P.op("dve", lambda e: e.tensor_tensor(scm[:], pSC[0:64, 0:256].rearrange("p (h t) -> p h t", h=4), m64, ALU.mult),
                     reads=[BN(B_SC), "cst"], writes=["scm"])
                for h in range(4):
                    P.op("pe", lambda e, h=h, tsl=tsl: e.matmul(pO[:, h * 64:(h + 1) * 64], Sbf[:, h, :], qe[:, h, tsl], start=True, stop=False),
                         reads=["Sbf", "qe"], writes=[BN(B_O)])
                    P.op("pe", lambda e, h=h, cc=cc: e.matmul(pO[:, h * 64:(h + 1) * 64], vtm[:, cc, h * 128:(h + 1) * 128], scm[:, h, :],
                                                             start=False, stop=True),
                         reads=["vtm", "scm"], writes=[BN(B_O)])
                P.op("act", lambda e, tsl=tsl: e.copy(osb[:, :, tsl], pO[:, 0:256].rearrange("p (h t) -> p h t", h=4)),
                     reads=[BN(B_O)], writes=["osb"])
                for h in range(4):
                    P.op("pe", lambda e, h=h, tsl=tsl: e.transpose(pKT[0:64, h * 128:(h + 1) * 128], kd[:, h, tsl], id_bf[:]),
                         reads=["kd", "id_bf"], writes=[BN(B_KT)])
                P.op("act", lambda e: e.copy(kdT[:].rearrange("p h k -> p (h k)"), pKT[0:64, 0:512]), reads=[BN(B_KT)], writes=["kdT"])
                for h in range(4):
                    P.op("pe", lambda e, h=h, cc=cc: e.matmul(pDS[:, h * 128:(h + 1) * 128], kdT[:, h, :], vtm[:, cc, h * 128:(h + 1) * 128],
                                                             start=True, stop=True), reads=["kdT", "vtm"], writes=[BN(B_DS)])
                ebl2 = bass.AP(eb, cc * 64 + 63, [[4 * T, 128], [T, 4], [0, 128]])
                P.op("dve", lambda e, ebl2=ebl2: e.tensor_tensor(Sst[:], Sst[:], ebl2, ALU.mult), reads=["Sst", "eb"], writes=["Sst"])
                P.op("dve", lambda e: e.tensor_tensor(Sst[:].rearrange("p h v -> p (h v)"), Sst[:].rearrange("p h v -> p (h v)"), pDS[:], ALU.add),
                     reads=["Sst", BN(B_DS)], writes=["Sst"])
                P.op("act", lambda e: e.copy(Sbf[:].rearrange("p h v -> p (h v)"), Sst[:].rearrange("p h v -> p (h v)")),
                     reads=["Sst"], writes=["Sbf"])
            P.op("act", lambda e: e.activation(fl(osq), fl(osb), AF.Square), reads=["osb"], writes=["osq"])
            for hp in range(2):
                b = bank()
                for hh in range(2):
                    h = 2 * hp + hh
                    P.op("pe", lambda e, h=h, hh=hh, b=b: e.matmul(banks[b][:, hh * T:(hh + 1) * T], ones_bf[:], osq[:, h, :], start=True, stop=True),
                         reads=["osq", "ones_bf"], writes=[BN(b)])
                P.op("act", lambda e, hp=hp, b=b: e.activation(ors[:, 2 * hp:2 * hp + 2, :].rearrange("p c t -> p (c t)"), banks[b][:], AF.Sqrt,
                                                               bias=eps_ap, scale=1.0 / 128), reads=[BN(b), "cst"], writes=["ors"])
            P.op("dve", lambda e: e.reciprocal(fl(ors), fl(ors)), reads=["ors"], writes=["ors"])
            for h in range(4):
                P.op("dve", lambda e, h=h: e.scalar_tensor_tensor(ytmp[:, h, :], osb[:, h, :], vec[:, V_HG + h:V_HG + h + 1], ors[:, h, :],
                                                                 ALU.mult, ALU.mult), reads=["osb", "ors", "vec"], writes=["ytmp"])
            P.op("pool", lambda e: e.tensor_tensor(fl(yb), fl(ytmp), fl(sg), ALU.mult), reads=["ytmp", "sg"], writes=["yb"])
            for ocp in range(4):
                b = bank()
                for oo in range(2):
                    oc = ocp * 2 + oo
                    for kc in range(8):
                        src = ya[:, kc, :] if kc < 4 else yb[:, kc - 4, :]
                        P.op("pe", lambda e, oc=oc, oo=oo, kc=kc, b=b, src=src: e.matmul(
                            banks[b][:, oo * T:(oo + 1) * T], w_oab_sb[:, kc, oc * 128:(oc + 1) * 128], src,
                            start=(kc == 0), stop=(kc == 7)), reads=["ya", "yb", "w_oab_sb"], writes=[BN(b)])
                for oo in range(2):
                    oc = ocp * 2 + oo
                    P.op("dve", lambda e, oc=oc, oo=oo, b=b: e.scalar_tensor_tensor(
                        xT[:, oc, :], banks[b][:, oo * T:(oo + 1) * T], gate1(0, oc), xT[:, oc, :], ALU.mult, ALU.add),
                        reads=[BN(b), "xT", "mod"], writes=["xT"])
            P.dma("sp", fm(x1T, i * T), xT[:], reads=["xT"], writes=["x1T_%d" % i], key="stA", final=dbg)
        P.barrier()

    if "A" in phases:
        _phA()

    def mlp_phase(l, src, src_pref, dst, dst_pref, tag, reverse_store, final_out, tiles):
        sb_off[0] = persist_end
        set_rot(list(range(8)))
        w1 = SB("w1", [128, 8, 4096], BF16)
        w2 = SB("w2", [128, 32, 1024], BF16)
        xTs = [SB("xTa", [128, 8, T]), SB("xTb", [128, 8, T])]
        sq = SB("sq", [128, 8, T], BF16)
        xn = SB("xn", [128, 8, T])
        rs = SB("rs", [128, T])
        hT = SB("hT", [128, 8, T], BF16)
        hid = SB("hid", [128, 32, T], BF16)
        rt = [SB("rt0", [128, 2 * T]), SB("rt1", [128, 2 * T])]
        xo = SB("xo", [128, 8, T]) if not final_out else SB("xo", [128, 2, 1024])
        load_w(w1, "w1", w1_in[l], 8, "w" + tag)
        load_w(w2, "w2", w2_in[l], 32, "w" + tag)

        def ld(i):
            s = i % 2
            P.dma("sp", xTs[s][:], fm(src, i * T), reads=["%s_%d" % (src_pref, i)], writes=["xT%d" % s], key="ld%s%d" % (tag, s))

        ld(tiles[0])
        for idx, i in enumerate(tiles):
            s = i % 2
            if idx + 1 < len(tiles):
                ld(tiles[idx + 1])
            xT = xTs[s]
            xname = "xT%d" % s
            rms_mod(xT, xname, hT, "hT", sq, xn, rs, A2, l, shift2)
            mlp_tile(l, hT, "hT", w1, w2, hid, rt, xT, xname)
            if final_out:
                P.op("dve", lambda e, xT=xT: e.tensor_copy(xn[:], xT[:, :, ::-1]), reads=[xname], writes=["xn"])
                for su in range(2):
                    for cq in range(2):
                        b = bank()
                        for c4 in range(4):
                            c = cq * 4 + c4
                            P.op("pe", lambda e, c=c, c4=c4, su=su, b=b: e.transpose(
                                banks[b][:, c4 * 128:(c4 + 1) * 128], xn[:, c, su * 128:(su + 1) * 128], ident),
                                reads=["xn", "cst"], writes=[BN(b)])
                        if cq == 0:
                            P.op("act", lambda e, su=su, cq=cq, b=b: e.copy(xo[:, su, cq * 512:(cq + 1) * 512], banks[b][:]),
                                 reads=[BN(b)], writes=["xo"])
                        else:
                            P.op("dve", lambda e, su=su, cq=cq, b=b: e.tensor_copy(xo[:, su, cq * 512:(cq + 1) * 512], banks[b][:]),
                                 reads=[BN(b)], writes=["xo"])
                for su in range(2):
                    r0 = S - (i + 1) * T + su * 128
                    P.dma("sp", dst[r0:r0 + 128, :], xo[:, su, :], reads=["xo"], key="stO", final=True)
            elif reverse_store:
                P.op("dve", lambda e, xT=xT: e.tensor_copy(xo[:], xT[:, :, ::-1]), reads=[xname], writes=["xo"])
                P.dma("sp", fm(dst, S - (i + 1) * T), xo[:], reads=["xo"], writes=["%s_%d" % (dst_pref, NT - 1 - i)], key="st" + tag, final=dbg)
            else:
                P.dma("sp", fm(dst, i * T), xT[:], reads=[xname], writes=["%s_%d" % (dst_pref, i)], key="st" + tag, final=dbg)
        P.barrier()

    if "B" in phases:
        mlp_phase(0, x1T, "x1T", x2T, "x2T", "B", True, False, tilesAB)

    def _phC():
        sb_off[0] = persist_end
        set_rot(list(range(8)))
        wq = SB("wq", [128, 8, 3072], BF16)
        xTs = [SB("xTa", [128, 8, T]), SB("xTb", [128, 8, T])]
        sq = SB("sq", [128, 8, T], BF16)
        xn = SB("xn", [128, 8, T])
        rs = SB("rs", [128, T])
        hT = SB("hT", [128, 8, T], BF16)
        sq2 = [SB("sq2a", [128, 2 * T], BF16), SB("sq2b", [128, 2 * T], BF16)]
        rr = [SB("rra", [128, 2 * T]), SB("rrb", [128, 2 * T])]
        qst = SB("qst", [128, 8, T], BF16)
        kst = SB("kst", [128, 8, T], BF16)
        vst = SB("vst", [128, 2, 1024], BF16)
        load_w(wq, "wq", w_qkv, 8, "wC")

        def ldc(i):
            s = i % 2
            P.dma("sp", xTs[s][:], fm(x2T, i * T), reads=["x2T_%d" % i], writes=["xT%d" % s], key="ldC%d" % s)

        ldc(tilesCDE[0])
        for idx, i in enumerate(tilesCDE):
            s = i % 2
            if idx + 1 < len(tilesCDE):
                ldc(tilesCDE[idx + 1])
            xT = xTs[s]
            xname = "xT%d" % s
            rms_mod(xT, xname, hT, "hT", sq, xn, rs, A1, 1, shift1)
            for j in range(8):
                b = bank()
                for qk in range(2):
                    for kc in range(8):
                        P.op("pe", lambda e, j=j, qk=qk, kc=kc, b=b: e.matmul(
                            banks[b][:, qk * T:(qk + 1) * T], wq[:, kc, qk * 1024 + j * 128: qk * 1024 + (j + 1) * 128], hT[:, kc, :],
                            start=(kc == 0), stop=(kc == 7)), reads=["hT", "wq"], writes=[BN(b)])
                a = j % 2
                P.op("act", lambda e, b=b, a=a: e.activation(sq2[a][:], banks[b][:], AF.Square), reads=[BN(b)], writes=["sq2%d" % a])
                b2 = bank()
                P.op("pe", lambda e, b2=b2, a=a: e.matmul(banks[b2][:], bd_bf[:], sq2[a][:], start=True, stop=True),
                     reads=["sq2%d" % a, "bd_bf"], writes=[BN(b2)])
                P.op("act", lambda e, b2=b2, a=a: e.activation(rr[a][:], banks[b2][:], AF.Sqrt, bias=eps_ap, scale=1.0 / 64),
                     reads=[BN(b2), "cst"], writes=["rr%d" % a])
                P.op("dve", lambda e, a=a: e.reciprocal(rr[a][:], rr[a][:]), reads=["rr%d" % a], writes=["rr%d" % a])
                P.op("dve", lambda e, b=b, a=a, j=j: e.scalar_tensor_tensor(qst[:, j, :], banks[b][:, 0:T], vec[:, V_QN:V_QN + 1], rr[a][:, 0:T],
                                                                           ALU.mult, ALU.mult), reads=[BN(b), "rr%d" % a, "vec"], writes=["qst"])
                P.op("dve", lambda e, b=b, a=a, j=j: e.scalar_tensor_tensor(kst[:, j, :], banks[b][:, T:2 * T], vec[:, V_KN:V_KN + 1], rr[a][:, T:2 * T],
                                                                           ALU.mult, ALU.mult), reads=[BN(b), "rr%d" % a, "vec"], writes=["kst"])
            for su in range(2):
                for hf in range(2):
                    b = bank()
                    for kc in range(8):
                        P.op("pe", lambda e, su=su, hf=hf, kc=kc, b=b: e.matmul(
                            banks[b][:], hT[:, kc, su * 128:(su + 1) * 128], wq[:, kc, 2048 + hf * 512: 2048 + (hf + 1) * 512],
                            start=(kc == 0), stop=(kc == 7)), reads=["hT", "wq"], writes=[BN(b)])
                    P.op("act", lambda e, su=su, hf=hf, b=b: e.copy(vst[:, su, hf * 512:(hf + 1) * 512], banks[b][:]),
                         reads=[BN(b)], writes=["vst"])
            P.dma("sp", fm(qTs, i * T), qst[:], reads=["qst"], writes=["qTs"], key="stC", final=dbg)
            P.dma("sp", fm(kTs, i * T), kst[:], reads=["kst"], writes=["kTs"], key="stC", final=dbg)
            P.dma("sp", vS[i * T:(i + 1) * T, :].rearrange("(s p) f -> p s f", p=128), vst[:], reads=["vst"], writes=["vS"], key="stC", final=dbg)
        P.barrier()

        sb_off[0] = persist_end
        ZB = [0, 1, 2]
        TB = [3, 4]
        B_OT = 5
        B_PO = [6, 7]
        qTl = [SB("qTl0", [128, S], BF16), SB("qTl1", [128, S], BF16)]
        kTl = [SB("kTl0", [128, S], BF16), SB("kTl1", [128, S], BF16)]
        vl = [SB("vl0", [128, 32, 128], BF16), SB("vl1", [128, 32, 128], BF16)]
        LA = 2
        NR = LA + 2
        om = [SB("om%d" % r, [128, 512]) for r in range(NR)]
        Pb = [SB("Pb%d" % r, [128, 516]) for r in range(NR)]
        wn = [SB("wn%d" % r, [128, 512], BF16) for r in range(NR)]
        wT = [SB("wT%d" % r, [128, 512], BF16) for r in range(NR)]
        ost = [SB("ost0", [128, 128], BF16), SB("ost1", [128, 128], BF16)]
        oTl = [SB("oTl0", [128, S], BF16), SB("oTl1", [128, S], BF16)]
        if first_only:
            for s_ in range(2):
                P.op("pool", lambda e, s_=s_: e.memset(oTl[s_][:], 0.0), writes=["oTl%d" % s_])

        def ld2(j):
            s = j % 2
            P.dma("sp", qTl[s][:], qTs[j], reads=["qTs"], writes=["qTl%d" % s], key="ldQ%d" % s)
            P.dma("sp", kTl[s][:], kTs[j], reads=["kTs"], writes=["kTl%d" % s], key="ldQ%d" % s)
            P.dma("sp", vl[s][:], vS[:, j * 128:(j + 1) * 128].rearrange("(b p) f -> p b f", p=128), reads=["vS"], writes=["vl%d" % s],
                  key="ldQ%d" % s)

        tl = []
        for j in range(8):
            for qi, qb in enumerate(qbs):
                for hh in range(2):
                    s0 = 128 * qb
                    ntile = (S - s0 + 511) // 512
                    for it in range(ntile):
                        k0 = s0 + 512 * it
                        tl.append(dict(j=j, s=j % 2, qb=qb, pi=qi % 2, hh=hh, it=it, ntile=ntile, k0=k0, W=min(512, S - k0), s0=s0,
                                       newpair=(qi == 0 and hh == 0 and it == 0),
                                       lastq=(qi == len(qbs) - 1 and hh == 1 and it == ntile - 1)))
        for n, t in enumerate(tl):
            t["n"] = n

        def stage1(t):
            n, s, W, it, hh = t["n"], t["s"], t["W"], t["it"], t["hh"]
            r = n % NR
            rp = (n - 1) % NR
            b = ZB[n % len(ZB)]
            pr = slice(64 * hh, 64 * hh + 64)
            s0, k0 = t["s0"], t["k0"]
            P.op("pe", lambda e: e.matmul(banks[b][:, 0:W], qTl[s][pr, s0:s0 + 128], kTl[s][pr, k0:k0 + W], start=True, stop=True),
                 reads=["qTl%d" % s, "kTl%d" % s], writes=[BN(b)])
            P.op("act", lambda e: e.activation(om[r][:, 0:W], banks[b][:, 0:W], AF.Sigmoid, scale=-0.125),
                 reads=[BN(b)], writes=["om%d" % r])
            if it == 0:
                P.op("dve", lambda e: e.tensor_tensor(om[r][:, 0:128], om[r][:, 0:128], tril, ALU.max),
                     reads=["om%d" % r, "cst"], writes=["om%d" % r])
                P.op("pool", lambda e: e.memset(Pb[r][:, 0:1], 1.0), writes=["Pc%d" % r])
                P.op("dve", lambda e: e.tensor_tensor_scan(Pb[r][:, 1:1 + W], om[r][:, 0:W], onesf[:, 0:W], 1.0, ALU.mult, ALU.mult),
                     reads=["om%d" % r, "cst"], writes=["Pb%d" % r])
            else:
                P.op("pool", lambda e: e.tensor_copy(Pb[r][:, 0:1], Pb[rp][:, 512:513]), reads=["Pb%d" % rp], writes=["Pc%d" % r])
                P.op("dve", lambda e: e.tensor_tensor_scan(Pb[r][:, 1:1 + W], om[r][:, 0:W], onesf[:, 0:W], Pb[rp][:, 512:513],
                                                           ALU.mult, ALU.mult),
                     reads=["om%d" % r, "Pb%d" % rp, "cst"], writes=["Pb%d" % r])
            P.op("dve", lambda e: e.scalar_tensor_tensor(wn[r][:, 0:W], om[r][:, 0:W], 1.0, Pb[r][:, 0:W], ALU.subtract, ALU.mult),
                 reads=["om%d" % r, "Pb%d" % r, "Pc%d" % r], writes=["wn%d" % r])

        def stage2(t):
            n, s, W, it, hh, pi, qb, ntile, k0 = t["n"], t["s"], t["W"], t["it"], t["hh"], t["pi"], t["qb"], t["ntile"], t["k0"]
            if t["newpair"] and t["j"] + 1 < 8:
                ld2(t["j"] + 1)
            r = n % NR
            b2 = TB[n % len(TB)]
            po = banks[B_PO[pi]]
            nu = W // 128
            for u in range(nu):
                P.op("pe", lambda e, u=u: e.transpose(banks_bf[b2][:, u * 128:(u + 1) * 128], wn[r][:, u * 128:(u + 1) * 128], id_bf[:]),
                     reads=["wn%d" % r, "id_bf"], writes=[BN(b2)])
            P.op("act", lambda e: e.copy(wT[r][:, 0:W], banks_bf[b2][:, 0:W]), reads=[BN(b2)], writes=["wT%d" % r])
            for u in range(nu):
                blk = k0 // 128 + u
                first = (it == 0 and u == 0)
                last = (it == ntile - 1 and u == nu - 1)
                P.op("pe", lambda e, u=u, blk=blk, first=first, last=last: e.matmul(
                    po[:, hh * 64:(hh + 1) * 64], wT[r][:, u * 128:(u + 1) * 128], vl[s][:, blk, hh * 64:(hh + 1) * 64],
                    start=first, stop=last, skip_group_check=True),
                    reads=["wT%d" % r, "vl%d" % s], writes=[BN(B_PO[pi])])
            if hh == 1 and it == ntile - 1:
                P.op("act", lambda e: e.mul(ost[pi][:], po[:, 0:128], -1.0), reads=[BN(B_PO[pi])], writes=["ost%d" % pi])
                P.op("pe", lambda e: e.transpose(banks_bf[B_OT][:, 0:128], ost[pi][:], id_bf[:]), reads=["ost%d" % pi, "id_bf"],
                     writes=[BN(B_OT)])
                P.op("act", lambda e: e.copy(oTl[s][:, qb * 128:(qb + 1) * 128], banks_bf[B_OT][:, 0:128]), reads=[BN(B_OT)],
                     writes=["oTl%d" % s])
                if t["lastq"]:
                    P.dma("sp", oTs[t["j"]], oTl[s][:], reads=["oTl%d" % s], writes=["oTs"], key="stQ", final=dbg)

        ld2(0)
        for n in range(len(tl) + LA):
            if n < len(tl):
                stage1(tl[n])
            if n - LA >= 0:
                stage2(tl[n - LA])
        P.barrier()

    if "C" in phases:
        _phC()

    def _phD():
        sb_off[0] = persist_end
        set_rot(list(range(8)))
        woc = SB("woc", [128, 8, 1024], BF16)
        xTs = [SB("xTa", [128, 8, T]), SB("xTb", [128, 8, T])]
        oTt = [SB("oTa", [128, 8, T], BF16), SB("oTb", [128, 8, T], BF16)]
        load_w(woc, "woc", w_oc, 8, "wD")

        def ldd(i):
            s = i % 2
            P.dma("sp", xTs[s][:], fm(x2T, i * T), reads=["x2T_%d" % i], writes=["xT%d" % s], key="ldD%d" % s)
            P.dma("sp", oTt[s][:], fm(oTs, i * T), reads=["oTs"], writes=["oT%d" % s], key="ldD%d" % s)

        ldd(tilesCDE[0])
        for idx, i in enumerate(tilesCDE):
            s = i % 2
            if idx + 1 < len(tilesCDE):
                ldd(tilesCDE[idx + 1])
            xT = xTs[s]
            for ocp in range(4):
                b = bank()
                for oo in range(2):
                    oc = ocp * 2 + oo
                    for kc in range(8):
                        P.op("pe", lambda e, oc=oc, oo=oo, kc=kc, b=b, s=s: e.matmul(
                            banks[b][:, oo * T:(oo + 1) * T], woc[:, kc, oc * 128:(oc + 1) * 128], oTt[s][:, kc, :],
                            start=(kc == 0), stop=(kc == 7)), reads=["oT%d" % s, "woc"], writes=[BN(b)])
                for oo in range(2):
                    oc = ocp * 2 + oo
                    P.op("dve", lambda e, oc=oc, oo=oo, b=b, xT=xT: e.scalar_tensor_tensor(
                        xT[:, oc, :], banks[b][:, oo * T:(oo + 1) * T], gate1(1, oc), xT[:, oc, :], ALU.mult, ALU.add),
                        reads=[BN(b), "xT%d" % s, "mod"], writes=["xT%d" % s])
            P.dma("sp", fm(x3T, i * T), xT[:], reads=["xT%d" % s], writes=["x3T_%d" % i], key="stD", final=dbg)
        P.barrier()

    if "D" in phases:
        _phD()

    if "E" in phases:
        mlp_phase(1, x3T, "x3T", out, "out", "E", False, True, tilesCDE)

    P.emit()
    return nc


_CACHE = {}


def kernel(x, c, ada_w, ada_b, norm_mix, norm_mlp, w_in_ab, conv_w, hg_norm, lb_logits, w_out_ab,
           w_qkv, q_norm, k_norm, w_out_c, mlp_w1, mlp_w2):
    f = lambda a: np.ascontiguousarray(np.asarray(a, dtype=np.float32))
    x = f(x)
    c = f(c)
    if "nc" not in _CACHE:
        _CACHE["nc"] = build()
    nc = _CACHE["nc"]
    consts = make_consts()
    vecs = make_vecs(f(ada_b), f(norm_mix), f(norm_mlp), f(conv_w), f(hg_norm), f(lb_logits), f(q_norm), f(k_norm))
    shared = {
        "constf": consts, "vecs": vecs, "ada_w": f(ada_w), "w_in_ab": f(w_in_ab)[0], "w_out_ab": f(w_out_ab)[0],
        "w_qkv": f(w_qkv)[0], "w_out_c": f(w_out_c)[0], "mlp_w1": f(mlp_w1), "mlp_w2": f(mlp_w2),
    }
    in_maps = []
    for b in range(8):
        m = dict(shared)
        m["x"] = x[b]
        m["cfm"] = np.ascontiguousarray(c[b].reshape(8, 128).T)
        in_maps.append(m)
    res = run_bass_kernel_spmd(nc, in_maps, core_ids=list(range(8)))
    return np.stack([np.asarray(r["out"], dtype=np.float32) for r in res.results], axis=0)
```
